# Optimizing a Trainium2 kernel written in Bass

```python
import math
import jax, jax.numpy as jnp
from jax import lax
import numpy as np

D_MODEL = 2048
BATCH = 4
SEQ = 4096
DEPTH = 1
DEC_BATCH = 32
DEC_SEQ = 64
PAST_LEN = 2048

CHUNK = 64
D_MIX = D_MODEL
D_RWKV = D_MIX // 2
D_S5 = D_MIX - D_RWKV
HEAD_SIZE = 64
N_HEADS = D_RWKV // HEAD_SIZE
DECAY_LORA = 64
AAA_LORA = 64
GATE_LORA = 160
N_SHIFT = 3 * D_RWKV + DECAY_LORA + AAA_LORA + GATE_LORA
N_IN = N_SHIFT + D_S5
S5_GROUP = 16
S5_GROUPS = D_S5 // S5_GROUP
S5_STATE = 64
D_FF = -(-8 * D_MODEL // (3 * 256)) * 256
ALPHA = (2 * DEPTH) ** 0.25
BETA = (8 * DEPTH) ** -0.25
LN_EPS = 1e-5
GN_EPS = 64e-5

kernel_name = 'rwkv7_s5_hymba_streaming_step'


def layer_norm(x, g, b):
    xf = x.astype(jnp.float32)
    mu = xf.mean(-1, keepdims=True)
    var = jnp.square(xf - mu).mean(-1, keepdims=True)
    y = (xf - mu) * lax.rsqrt(var + LN_EPS) * g.astype(jnp.float32) + b.astype(jnp.float32)
    return y.astype(x.dtype)


def wkv7_scan(r, decay, k, v, kk, a, s0):
    def step(s, inp):
        r_t, w_t, k_t, v_t, kk_t, a_t = inp
        sa = jnp.einsum('bhij,bhj->bhi', s, kk_t)
        s = (s * w_t[:, :, None, :]
             - sa[..., None] * (kk_t * a_t)[:, :, None, :]
             + v_t[..., None] * k_t[:, :, None, :])
        return s, jnp.einsum('bhij,bhj->bhi', s, r_t)
    xs = tuple(jnp.moveaxis(t, 1, 0) for t in (r, decay, k, v, kk, a))
    s_last, ys = lax.scan(step, s0, xs)
    return jnp.moveaxis(ys, 0, 1), s_last


def _affine_combine(e1, e2):
    a1, b1 = e1
    a2, b2 = e2
    return a1 * a2, a2 * b1 + b2


def s5_scan(u, lam_bar, b_bar, c_mat, h0):
    bsz, seq = u.shape[:2]
    c = min(CHUNK, seq)
    n_chunks = seq // c
    u_ch = jnp.moveaxis(u.reshape(bsz, n_chunks, c, S5_GROUPS, S5_GROUP), 1, 0)

    def chunk_step(h, u_c):
        bu = jnp.einsum('bcgk,gpk->bcgp', u_c.astype(jnp.complex64), b_bar)
        bu = bu.at[:, 0].add(lam_bar * h)
        dec = jnp.broadcast_to(lam_bar, bu.shape)
        _, hs = lax.associative_scan(_affine_combine, (dec, bu), axis=1)
        y = jnp.einsum('bcgp,gkp->bcgk', hs, c_mat).real
        return hs[:, -1], y

    h_last, ys = lax.scan(chunk_step, h0, u_ch)
    return jnp.moveaxis(ys, 0, 1).reshape(u.shape), h_last


def mixer(xn, shift0, s0, h0, w_in, mu_shift, w0, w_lora_up, a0, a_lora_up, g_lora_up,
          k_k, k_a, r_k, gn_g, gn_b, a_re, a_im, log_dt, b_re, b_im, c_re, c_im,
          s5_d, glu_w, glu_b, w_out):
    f32 = lambda t: t.astype(jnp.float32)
    bsz, seq, _ = xn.shape
    proj = jnp.einsum('bld,de->ble', xn, w_in)
    p_rw = proj[..., :N_SHIFT]
    u = proj[..., N_SHIFT:]
    prev = jnp.concatenate([shift0.astype(p_rw.dtype), p_rw[:, :-1]], axis=1)
    p_s = p_rw + (prev - p_rw) * mu_shift
    new_shift = p_rw[:, -1:]
    o1 = D_RWKV
    o2 = 2 * D_RWKV
    o3 = 3 * D_RWKV
    o4 = o3 + DECAY_LORA
    o5 = o4 + AAA_LORA
    r, k, v, xw, xa, xg = jnp.split(f32(p_s), [o1, o2, o3, o4, o5], axis=-1)
    w = f32(w0) + jnp.tanh(xw) @ f32(w_lora_up)
    w = -jax.nn.softplus(-w) - 0.5
    decay = jnp.exp(-jnp.exp(w))
    a = jax.nn.sigmoid(f32(a0) + xa @ f32(a_lora_up))
    g = jax.nn.sigmoid(xg) @ f32(g_lora_up)
    heads = lambda t: t.reshape(bsz, seq, N_HEADS, HEAD_SIZE)
    kk = heads(k * f32(k_k))
    kk = kk / jnp.maximum(jnp.sqrt(jnp.sum(kk * kk, -1, keepdims=True)), 1e-12)
    k = k * (1.0 + (a - 1.0) * f32(k_a))
    rh, kh, vh = heads(r), heads(k), heads(v)
    yh, s_last = wkv7_scan(rh, heads(decay), kh, vh, kk, heads(a), f32(s0))
    mu = yh.mean(-1, keepdims=True)
    var = jnp.square(yh - mu).mean(-1, keepdims=True)
    yn = ((yh - mu) * lax.rsqrt(var + GN_EPS)).reshape(bsz, seq, D_RWKV) * f32(gn_g) + f32(gn_b)
    bonus = (jnp.sum(rh * kh * f32(r_k), -1, keepdims=True) * vh).reshape(bsz, seq, D_RWKV)
    o_rwkv = (yn + bonus) * g
    dt = jnp.exp(f32(log_dt))[:, None]
    lam = lax.complex(f32(a_re), f32(a_im))
    lam_bar = jnp.exp(lam * dt)
    b_bar = ((lam_bar - 1.0) / lam)[..., None] * lax.complex(f32(b_re), f32(b_im))
    c_mat = lax.complex(f32(c_re), f32(c_im))
    ug = f32(u).reshape(bsz, seq, S5_GROUPS, S5_GROUP)
    ys, h_last = s5_scan(ug, lam_bar, b_bar, c_mat, h0)
    ys = ys + f32(s5_d) * ug
    z = jax.nn.gelu(ys.reshape(bsz, seq, D_S5))
    o_s5 = z * jax.nn.sigmoid(z @ f32(glu_w) + f32(glu_b))
    o = jnp.concatenate([o_rwkv, o_s5], axis=-1).astype(xn.dtype)
    return o @ w_out, new_shift, s_last, h_last


def swiglu(x, w1, w3, w2):
    return (jax.nn.silu(x @ w1) * (x @ w3)) @ w2


def _normal(key, shape, scale):
    return jax.random.normal(key, shape, jnp.float32) * scale


def setup_inputs(seed: int = 0) -> dict:
    key = jax.random.key(seed)
    ks = iter(jax.random.split(key, 48))
    nk = lambda: next(ks)
    f = jnp.float32
    L = DEPTH
    w0_base = jnp.linspace(-6.0, -1.0, D_RWKV, dtype=f)
    a_im_base = math.pi * jnp.arange(S5_STATE, dtype=f)
    return {
        'x_prompt': _normal(nk(), (BATCH, SEQ, D_MODEL), 1.0),
        'x_sample': _normal(nk(), (DEC_BATCH, DEC_SEQ, D_MODEL), 1.0),
        'cache_shift': _normal(nk(), (L, DEC_BATCH, 1, N_SHIFT), 1.0),
        'state_wkv': _normal(nk(), (L, DEC_BATCH, N_HEADS, HEAD_SIZE, HEAD_SIZE), 0.5),
        'state_s5_re': _normal(nk(), (L, DEC_BATCH, S5_GROUPS, S5_STATE), 0.5),
        'state_s5_im': _normal(nk(), (L, DEC_BATCH, S5_GROUPS, S5_STATE), 0.5),
        'ln_in_g': 1.0 + _normal(nk(), (D_MODEL,), 0.02),
        'ln_in_b': _normal(nk(), (D_MODEL,), 0.02),
        'w_in': _normal(nk(), (L, D_MODEL, N_IN), D_MODEL ** -0.5),
        'mu_shift': jax.random.uniform(nk(), (L, N_SHIFT), f, 0.0, 1.0),
        'w0': w0_base + _normal(nk(), (L, D_RWKV), 0.1),
        'w_lora_up': _normal(nk(), (L, DECAY_LORA, D_RWKV), 0.1 * DECAY_LORA ** -0.5),
        'a0': _normal(nk(), (L, D_RWKV), 0.1),
        'a_lora_up': _normal(nk(), (L, AAA_LORA, D_RWKV), 0.1 * AAA_LORA ** -0.5),
        'g_lora_up': _normal(nk(), (L, GATE_LORA, D_RWKV), GATE_LORA ** -0.5),
        'k_k': 0.85 + _normal(nk(), (L, D_RWKV), 0.02),
        'k_a': 1.0 + _normal(nk(), (L, D_RWKV), 0.02),
        'r_k': _normal(nk(), (L, N_HEADS, HEAD_SIZE), 0.1),
        'gn_g': 1.0 + _normal(nk(), (L, D_RWKV), 0.02),
        'gn_b': _normal(nk(), (L, D_RWKV), 0.02),
        's5_a_re': -0.5 + _normal(nk(), (L, S5_GROUPS, S5_STATE), 0.01),
        's5_a_im': a_im_base + _normal(nk(), (L, S5_GROUPS, S5_STATE), 0.01),
        's5_log_dt': jax.random.uniform(nk(), (L, S5_GROUPS), f, math.log(1e-3), math.log(1e-1)),
        's5_b_re': _normal(nk(), (L, S5_GROUPS, S5_STATE, S5_GROUP), (2 * S5_GROUP) ** -0.5),
        's5_b_im': _normal(nk(), (L, S5_GROUPS, S5_STATE, S5_GROUP), (2 * S5_GROUP) ** -0.5),
        's5_c_re': _normal(nk(), (L, S5_GROUPS, S5_GROUP, S5_STATE), (2 * S5_STATE) ** -0.5),
        's5_c_im': _normal(nk(), (L, S5_GROUPS, S5_GROUP, S5_STATE), (2 * S5_STATE) ** -0.5),
        's5_d': _normal(nk(), (L, S5_GROUPS, S5_GROUP), 1.0),
        'glu_w': _normal(nk(), (L, D_S5, D_S5), D_S5 ** -0.5),
        'glu_b': _normal(nk(), (L, D_S5), 0.01),
        'w_out': _normal(nk(), (L, D_MIX, D_MODEL), BETA * D_MIX ** -0.5),
        'ln1_g': 1.0 + _normal(nk(), (L, D_MODEL), 0.02),
        'ln1_b': _normal(nk(), (L, D_MODEL), 0.02),
        'ffn_w1': _normal(nk(), (L, D_MODEL, D_FF), D_MODEL ** -0.5),
        'ffn_w3': _normal(nk(), (L, D_MODEL, D_FF), D_MODEL ** -0.5),
        'ffn_w2': _normal(nk(), (L, D_FF, D_MODEL), BETA * D_FF ** -0.5),
        'ln2_g': 1.0 + _normal(nk(), (L, D_MODEL), 0.02),
        'ln2_b': _normal(nk(), (L, D_MODEL), 0.02),
    }


def reference(x_prompt, x_sample, cache_shift, state_wkv, state_s5_re, state_s5_im,
              ln_in_g, ln_in_b, w_in, mu_shift, w0, w_lora_up, a0, a_lora_up, g_lora_up,
              k_k, k_a, r_k, gn_g, gn_b, s5_a_re, s5_a_im, s5_log_dt, s5_b_re, s5_b_im,
              s5_c_re, s5_c_im, s5_d, glu_w, glu_b, w_out, ln1_g, ln1_b,
              ffn_w1, ffn_w3, ffn_w2, ln2_g, ln2_b):
    assert x_sample.shape[1] <= CHUNK

    def run(x, shift, wkv, h_re, h_im):
        x = layer_norm(x, ln_in_g, ln_in_b)
        outs = []
        for l in range(DEPTH):
            h0 = lax.complex(h_re[l].astype(jnp.float32), h_im[l].astype(jnp.float32))
            mix, n_shift, n_wkv, n_h = mixer(
                x, shift[l], wkv[l], h0, w_in[l], mu_shift[l], w0[l], w_lora_up[l], a0[l],
                a_lora_up[l], g_lora_up[l], k_k[l], k_a[l], r_k[l], gn_g[l], gn_b[l],
                s5_a_re[l], s5_a_im[l], s5_log_dt[l], s5_b_re[l], s5_b_im[l], s5_c_re[l],
                s5_c_im[l], s5_d[l], glu_w[l], glu_b[l], w_out[l])
            x = layer_norm(ALPHA * x + mix, ln1_g[l], ln1_b[l])
            x = layer_norm(ALPHA * x + swiglu(x, ffn_w1[l], ffn_w3[l], ffn_w2[l]), ln2_g[l], ln2_b[l])
            outs.append((n_shift, n_wkv, jnp.real(n_h), jnp.imag(n_h)))
        dt = x.dtype
        stack = lambda i: jnp.stack([o[i] for o in outs]).astype(dt)
        return x, stack(0), stack(1), stack(2), stack(3)

    bp = x_prompt.shape[0]
    dtp = x_prompt.dtype
    y_prompt, shift_p, wkv_p, s5re_p, s5im_p = run(
        x_prompt,
        jnp.zeros((DEPTH, bp, 1, N_SHIFT), dtp),
        jnp.zeros((DEPTH, bp, N_HEADS, HEAD_SIZE, HEAD_SIZE), dtp),
        jnp.zeros((DEPTH, bp, S5_GROUPS, S5_STATE), dtp),
        jnp.zeros((DEPTH, bp, S5_GROUPS, S5_STATE), dtp))
    y_sample, shift_s, wkv_s, s5re_s, s5im_s = run(
        x_sample, cache_shift, state_wkv, state_s5_re, state_s5_im)
    return (y_prompt, y_sample, shift_p, wkv_p, s5re_p, s5im_p, shift_s, wkv_s, s5re_s, s5im_s)
```

```python
import math
import contextlib
import numpy as np
import concourse.bass as bass
import concourse.mybir as mybir
from concourse.bass_utils import run_bass_kernel_spmd

F32 = mybir.dt.float32
I32 = mybir.dt.int32
BF16 = mybir.dt.bfloat16
AF = mybir.ActivationFunctionType
ALU = mybir.AluOpType

D = 2048
DFF = 5632
NSH = 3360
NIN = 4384
T = 128
L = 64
NCH = T // L
PAD = 64
ALPHA = 2.0 ** 0.25
LN_EPS = 1e-5
GN_EPS = 64e-5
C0 = math.exp(-0.5)


class Op:
    __slots__ = ("eng", "fn", "r", "w", "stream", "deps", "sig", "idx", "cnt")

    def __init__(self, eng, fn, r, w, stream):
        self.eng, self.fn, self.r, self.w, self.stream = eng, fn, r, w, stream
        self.deps = []
        self.sig = False
        self.cnt = 0


def _pref(k):
    return k[0] if isinstance(k, tuple) else k


class KB:
    ENGS = ("pe", "dve", "act", "pool", "sp")

    def __init__(self, nc):
        self.nc = nc
        self.ops = []
        self.last_w = {}
        self.readers = {}
        self.floor = {}
        self.streams = []

    def op(self, eng, fn, r=(), w=(), stream=None):
        o = Op(eng, fn, tuple(r), tuple(w), stream)
        o.idx = len(self.ops)
        deps = set()
        for k in o.r + o.w:
            p = self.last_w.get(k)
            if p is None:
                p = self.floor.get(_pref(k))
            if p is not None:
                deps.add(p)
        for k in o.w:
            for q in self.readers.get(k, ()):
                deps.add(q)
        deps.discard(o.idx)
        o.deps = sorted(deps)
        for k in o.w:
            self.last_w[k] = o.idx
            self.readers[k] = []
        for k in o.r:
            if k not in o.w:
                self.readers.setdefault(k, []).append(o.idx)
        if stream is not None and stream not in self.streams:
            self.streams.append(stream)
        self.ops.append(o)
        return o

    def fence(self, prefixes, eng, fn):
        keys = [k for k in set(self.last_w) | set(self.readers) if _pref(k) in prefixes]
        o = self.op(eng, fn, r=(), w=keys)
        for k in keys:
            self.last_w.pop(k, None)
            self.readers.pop(k, None)
        for p in prefixes:
            self.floor[p] = o.idx
        return o

    def mm(self, out, lhsT, rhs, start=True, stop=True, r=(), w=(), **kw):
        return self.op("pe", lambda e: e.matmul(out, lhsT, rhs, start=start, stop=stop, **kw), r, w)

    def tr(self, out, in_, ident, r=(), w=()):
        return self.op("pe", lambda e: e.transpose(out, in_, ident), r, w)

    def dma(self, q, out, in_, r=(), w=(), stream=None):
        return self.op(q, lambda e: e.dma_start(out=out, in_=in_), r, w, stream)

    def act(self, out, in_, func, r=(), w=(), **kw):
        return self.op("act", lambda e: e.activation(out=out, in_=in_, func=func, **kw), r, w)

    def tt(self, eng, out, a, b, op, r=(), w=()):
        return self.op(eng, lambda e: e.tensor_tensor(out=out, in0=a, in1=b, op=op), r, w)

    def ts(self, eng, out, a, s1, op0, s2=None, op1=None, r=(), w=()):
        if op1 is None:
            return self.op(eng, lambda e: e.tensor_scalar(out=out, in0=a, scalar1=s1, scalar2=None, op0=op0), r, w)
        return self.op(eng, lambda e: e.tensor_scalar(out=out, in0=a, scalar1=s1, scalar2=s2, op0=op0, op1=op1), r, w)

    def stt(self, out, a, s, b, op0, op1, r=(), w=()):
        return self.op("dve", lambda e: e.scalar_tensor_tensor(out=out, in0=a, scalar=s, in1=b, op0=op0, op1=op1), r, w)

    def cp(self, eng, out, in_, r=(), w=()):
        if eng == "act":
            return self.act(out, in_, AF.Copy, r, w)
        return self.op(eng, lambda e: e.tensor_copy(out=out, in_=in_), r, w)

    def emit(self):
        nc = self.nc
        ops = self.ops
        is_dma = lambda o: o.stream is not None

        def skip(p, o):
            return (not is_dma(p)) and (not is_dma(o)) and p.eng == "pe" and o.eng == "pe"

        for o in ops:
            for d in o.deps:
                p = ops[d]
                if not skip(p, o):
                    p.sig = True
        cnt = {e: 0 for e in self.ENGS}
        scnt = {s: 0 for s in self.streams}
        for o in ops:
            if is_dma(o):
                o.sig = True
                scnt[o.stream] += 16
                o.cnt = scnt[o.stream]
            elif o.sig:
                cnt[o.eng] += 1
                o.cnt = cnt[o.eng]
        engmap = {"pe": "tensor", "dve": "vector", "act": "scalar", "pool": "gpsimd", "sp": "sync"}
        with contextlib.ExitStack() as st:
            sems = {e: st.enter_context(nc.semaphore("s_" + e)) for e in self.ENGS}
            ssems = {s: st.enter_context(nc.semaphore("d_%d" % i)) for i, s in enumerate(self.streams)}
            block = st.enter_context(nc.Block())
            per_eng = {e: [o for o in ops if o.eng == e] for e in self.ENGS}
            out_streams = [s for s in self.streams if isinstance(s, tuple) and s[0] == "out"]

            def body(ename):
                def f(eng):
                    seen = {}
                    for o in per_eng[ename]:
                        for d in o.deps:
                            p = ops[d]
                            if is_dma(p):
                                sem, val = ssems[p.stream], p.cnt
                            else:
                                if skip(p, o):
                                    continue
                                sem, val = sems[p.eng], p.cnt
                            key = id(sem)
                            if seen.get(key, 0) >= val:
                                continue
                            seen[key] = val
                            eng.wait_ge(sem, val)
                        ins = o.fn(eng)
                        if is_dma(o):
                            ins.then_inc(ssems[o.stream], 16)
                        elif o.sig:
                            ins.then_inc(sems[o.eng], 1)
                    if ename == "sp":
                        for s in out_streams:
                            eng.wait_ge(ssems[s], scnt[s])
                return f

            for e in self.ENGS:
                if per_eng[e] or e == "sp":
                    getattr(block, engmap[e])(body(e))
        return nc


CO_MU, CO_W0, CO_A0, CO_KK, CO_KA, CO_RK, CO_GG, CO_GB, CO_SD, CO_GLB, CO_OMKA = 0, 27, 35, 43, 51, 59, 67, 75, 83, 91, 99
NCOL = 107
K_ID, K_BO, K_MASK, K_LT, K_CM, K_TAU, K_R0, K_PI2 = 0, 128, 256, 384, 448, 576, 640, 704
NCONST = 705


import os
STOP = int(os.environ.get("KSTOP", "9"))
SUB = int(os.environ.get("KSUB", "9"))
SUBB = int(os.environ.get("KSUBB", "9"))
KS5 = int(os.environ.get("KS5", "9"))


def build(NPRE, NOWN, NSAMP_TILES=2):
    nc = bass.Bass("TRN2", target_bir_lowering=False)
    kb = KB(nc)
    NTP = NPRE + NOWN
    di = lambda n, s: nc.dram_tensor(n, s, F32, kind="ExternalInput").ap()
    do = lambda n, s: nc.dram_tensor(n, s, F32, kind="ExternalOutput").ap()
    xp = di("xp", [NTP * T, D]); xs = di("xs", [NSAMP_TILES * T, D]); flag = di("flag", [128, 1])
    cshift = di("cshift", [2 * NSAMP_TILES, 27 * 128])
    swkv = di("swkv", [2 * NSAMP_TILES, 128, 8 * 64])
    s5st = di("s5st", [2 * NSAMP_TILES, 2, 128, 32])
    w_in = di("w_in", [D, NIN]); w_out = di("w_out", [D, D]); w1 = di("w1", [D, DFF]); w3 = di("w3", [D, DFF])
    w2 = di("w2", [DFF, D]); glu_w = di("glu_w", [1024, 1024])
    lora_d = di("lora", [3, 128, 1024])
    cols_d = di("cols", [128, NCOL]); consts_d = di("consts", [128, NCONST]); rows_d = di("rows", [6, D])
    s5a_d = di("s5a", [128, 3 * 32]); s5q_d = di("s5q", [128, 3 * 1024])
    bblk_d = di("bblk", [128, 2 * 1024]); cblk_d = di("cblk", [128, 2 * 1024])
    NSQ = 2 * NSAMP_TILES + 1
    scr = lambda n, shp: nc.dram_tensor(n, shp, BF16, kind="Internal").ap()
    w_out_b = scr("w_out_b", [D, D]); w2_b = scr("w2_b", [DFF, D])

    def mkblocked(name, R, blocks):
        tbl = {}
        off = 0
        for c0, ncol in blocks:
            tbl[c0] = (off, ncol)
            off += R * ncol
        return (scr(name, [off]), tbl, R // 128)

    w_in_blocks = [(i * 256, 256) for i in range(12)] + [(3072, 256), (3328, 32)] + [(3360 + i * 256, 256) for i in range(4)]
    ffn_blocks = [((qt * 11 + fl) * 128, 256 if fl < 10 else 128) for qt in range(4) for fl in range(0, 11, 2)]
    w_in_b = mkblocked("w_in_b", D, w_in_blocks); w1_b = mkblocked("w1_b", D, ffn_blocks); w3_b = mkblocked("w3_b", D, ffn_blocks)
    glu_b_ = mkblocked("glu_w_b", 1024, [(i * 256, 256) for i in range(4)])
    yp = do("yp", [NOWN * T, D]); ys_o = do("ys", [NSAMP_TILES * T, D])
    shift_o = do("shift_o", [NSQ, 27, 128]); wkv_o = do("wkv_o", [NSQ, 128, 512]); s5_o = do("s5_o", [NSQ, 2, 128, 32])

    st = contextlib.ExitStack()
    sb = lambda n, s, dt=F32: st.enter_context(nc.sbuf_tensor(n, s, dt))
    with st:
        CONST = sb("CONST", [128, NCONST]); COLS = sb("COLS", [128, NCOL]); FLAG = sb("FLAG", [128, 1]); NCOLS = sb("NCOLS", [128, NCOL])
        LORA = sb("LORA", [128, 3, 1024])
        COS = sb("COS", [128, 32, 64]); SIN = sb("SIN", [128, 32, 64]); RHO0 = sb("RHO0", [128, 32, 64])
        RHO = sb("RHO", [128, 32])
        BRE = sb("BRE", [128, 1024]); BIM = sb("BIM", [128, 1024]); CRE = sb("CRE", [128, 1024]); NCIM = sb("NCIM", [128, 1024])
        XTS = [sb("XT0", [128, D]), sb("XT1", [128, D])]; XF = sb("XF", [128, 16, T], BF16)
        WS = [sb("WS0", [128, 4096]), sb("WS1", [128, 4096])]
        ROWS = sb("ROWS", [128, 1, D])
        PR = sb("PR", [128, 3, 8, PAD + T])
        AR_ = sb("ARENA", [128, 6144]); AR2 = sb("ARENA2", [128, 5632])
        UF = sb("UF", [128, 8, T]); OF = sb("OF", [128, 16, T], BF16)
        STT_ = sb("STATE", [128, 8, 64]); HC = sb("HC", [128, 2, 32])
        PREV = sb("PREV", [128, 27]); SH0 = sb("SH0", [128, 27, 2]); SHO = sb("SHO", [128, 27, NSQ])
        SMALL = sb("SMALL", [128, 96]); STATS = sb("STATS", [128, 4, 6])
        PSA = st.enter_context(nc.psum_tensor("PSA", [128, 2048], F32))
        PSB = st.enter_context(nc.psum_tensor("PSB", [128, 2048], F32))

        def bank(b):
            return (PSA if b < 4 else PSB)[:, (b % 4) * 512:(b % 4) * 512 + 512]

        bank_rr = [0]
        STk = [("ST", h_) for h_ in range(8)]
        HCk = [("HC", q_) for q_ in range(4)]

        def nb():
            bank_rr[0] = (bank_rr[0] + 1) % 6
            return bank_rr[0]

        ident = CONST[:, K_ID:K_ID + 128]; bones = CONST[:, K_BO:K_BO + 128]; MASK = CONST[:, K_MASK:K_MASK + 128]
        LT = CONST[0:64, K_LT:K_LT + 64]; CM = CONST[:, K_CM:K_CM + 128]; TAU = CONST[:, K_TAU:K_TAU + 64]
        R0M = CONST[:, K_R0:K_R0 + 64]; PI2 = CONST[:, K_PI2:K_PI2 + 1]
        col = lambda o, i=0, n=1: COLS[:, o + i:o + i + n]

        ld = [0]

        def load(q, dst, src, key):
            ld[0] += 1
            kb.dma(q, dst, src, w=[key], stream=("ld", ld[0]))

        load("sp", CONST[:], consts_d, "CONST"); load("sp", COLS[:], cols_d, "COLS"); load("sp", FLAG[:], flag, "FLAG")
        load("sp", LORA[:], lora_d.rearrange("a p c -> p a c"), "LORA")
        kb.ts("dve", COLS[:, CO_OMKA:CO_OMKA + 8], COLS[:, CO_KA:CO_KA + 8], -1.0, ALU.mult, 1.0, ALU.add, r=["COLS"], w=["COLS"])
        kb.ts("dve", NCOLS[:], COLS[:], -1.0, ALU.mult, r=["COLS"], w=["COLS"])
        ncol_ = lambda o, i=0: NCOLS[:, o + i:o + i + 1]

        def sigm(eng2, out, in_, r, w, negbias=None, scale=1.0):
            if negbias is None:
                kb.act(out, in_, AF.Exp, r=r, w=w, scale=-scale)
            else:
                kb.act(out, in_, AF.Exp, r=r + ["COLS"], w=w, scale=-scale, bias=negbias)
            kb.ts(eng2, out, out, 1.0, ALU.add, r=w, w=w)
            kb.op("dve", lambda e: e.reciprocal(out=out, in_=out), r=w, w=w)

        def rsqrt_(out, in_, r, w):
            kb.act(out, in_, AF.Ln, r=r, w=w)
            kb.act(out, out, AF.Exp, r=w, w=w, scale=-0.5)
        PRf = PR[:].rearrange("p a b c -> p (a b c)")
        S5A = PRf[:, 0:96]; tmpa = [PRf[:, 128 + i * 32:128 + (i + 1) * 32] for i in range(6)]
        load("sp", S5A, s5a_d, ("PR", "s5a"))
        arr = [AR_[:, i * 1024:(i + 1) * 1024] for i in range(6)]
        prr = [PRf[:, 512 + i * 1024:512 + (i + 1) * 1024] for i in range(4)]
        wsr = [WS[0][:, i * 1024:(i + 1) * 1024] for i in range(4)] + [WS[1][:, i * 1024:(i + 1) * 1024] for i in range(4)]
        load("sp", WS[0][:, 0:3072], s5q_d, ("WS", 0))
        load("sp", WS[1][:, 0:2048], bblk_d, ("WS", 1))
        load("sp", CRE[:], cblk_d[:, 0:1024], "CRE"); load("sp", NCIM[:], cblk_d[:, 1024:2048], "NCIM")
        kb.ts("pool", NCIM[:], NCIM[:], -1.0, ALU.mult, r=["NCIM"], w=["NCIM"])
        kb.op("pool", lambda e: e.memset(PR[:], 0.0), w=[("PR", "all")])

        def sincos(src, n, frac, tmpf, sin_out, cos_out, kin, kout, ktmp):
            kb.ts("dve", frac, src, 1.0 / (2 * math.pi), ALU.mult, r=kin, w=ktmp)
            kb.cp("dve", tmpf.bitcast(I32), frac, r=ktmp, w=ktmp)
            kb.cp("dve", tmpf, tmpf.bitcast(I32), r=ktmp, w=ktmp)
            kb.tt("dve", frac, frac, tmpf, ALU.subtract, r=ktmp, w=ktmp)
            kb.ts("dve", tmpf, frac, 0.5, ALU.is_gt, r=ktmp, w=ktmp)
            kb.tt("dve", frac, frac, tmpf, ALU.subtract, r=ktmp, w=ktmp)
            kb.ts("dve", tmpf, frac, -0.5, ALU.is_lt, r=ktmp, w=ktmp)
            kb.tt("dve", frac, frac, tmpf, ALU.add, r=ktmp, w=ktmp)
            kb.act(sin_out, frac, AF.Sin, r=ktmp, w=kout, scale=2 * math.pi)
            kb.ts("dve", tmpf, frac, -1.0, ALU.mult, r=ktmp, w=ktmp)
            kb.tt("dve", tmpf, tmpf, frac, ALU.max, r=ktmp, w=ktmp)
            kb.act(cos_out, tmpf, AF.Sin, r=ktmp + ["CONST"], w=kout, scale=-2 * math.pi, bias=PI2)

        kS = [("PR", "s5a")]
        a_re, a_im, ldt = S5A[:, 0:32], S5A[:, 32:64], S5A[:, 64:96]
        dt_, ard, th = tmpa[0], tmpa[1], tmpa[2]
        kb.act(dt_, ldt, AF.Exp, r=kS, w=[("PR", "t0")])
        kb.tt("dve", ard, a_re, dt_, ALU.mult, r=kS + [("PR", "t0")], w=[("PR", "t1")])
        kb.act(RHO[:], ard, AF.Exp, r=[("PR", "t1")], w=["RHO"])
        kb.tt("dve", th, a_im, dt_, ALU.mult, r=kS + [("PR", "t0")], w=[("PR", "t2")])
        ANG = AR_[:, 0:2048]
        kb.tt("dve", ANG.rearrange("p (a b) -> p a b", a=32), th.unsqueeze(2).broadcast_to([128, 32, 64]),
              TAU.unsqueeze(1).broadcast_to([128, 32, 64]), ALU.mult, r=[("PR", "t2"), "CONST"], w=[("ARENA", "ang")])
        sincos(ANG, 2048, AR_[:, 2048:4096], AR_[:, 4096:6144], SIN[:].rearrange("p a b -> p (a b)"),
               COS[:].rearrange("p a b -> p (a b)"), [("ARENA", "ang")], ["SINCOS"], [("ARENA", "sc")])
        kb.tt("dve", RHO0[:], RHO[:].unsqueeze(2).broadcast_to([128, 32, 64]), R0M.unsqueeze(1).broadcast_to([128, 32, 64]),
              ALU.mult, r=["RHO", "CONST"], w=["RHO0"])
        kq = [("WS", 0)]
        areq, aimq, ldq = wsr[0], wsr[1], wsr[2]
        dtq, rhoq, thq, sinq, cosq, t1_, t2_ = prr[0], prr[1], prr[2], prr[3], wsr[3], wsr[6], wsr[7]
        kA = [("ARENA", "q")]
        kb.fence({"ARENA"}, "dve", lambda e: e.memset(SMALL[:, 0:1], 0.0))
        kb.act(dtq, ldq, AF.Exp, r=kq, w=[("PR", "dtq")])
        kb.tt("dve", rhoq, areq, dtq, ALU.mult, r=kq + [("PR", "dtq")], w=[("PR", "rhoq")])
        kb.act(rhoq, rhoq, AF.Exp, r=[("PR", "rhoq")], w=[("PR", "rhoq")])
        kb.tt("dve", thq, aimq, dtq, ALU.mult, r=kq + [("PR", "dtq")], w=[("PR", "thq")])
        sincos(thq, 1024, arr[0], arr[1], sinq, cosq, [("PR", "thq")], [("PR", "sinq"), ("WS", "cosq")], kA)
        nr, ni, den = arr[2], arr[3], arr[4]
        kb.tt("dve", nr, rhoq, cosq, ALU.mult, r=[("PR", "rhoq"), ("WS", "cosq")], w=[("ARENA", "nr")])
        kb.ts("dve", nr, nr, -1.0, ALU.add, r=[("ARENA", "nr")], w=[("ARENA", "nr")])
        kb.tt("dve", ni, rhoq, sinq, ALU.mult, r=[("PR", "rhoq"), ("PR", "sinq")], w=[("ARENA", "ni")])
        kb.tt("dve", den, areq, areq, ALU.mult, r=kq, w=[("ARENA", "den")])
        kb.tt("dve", arr[5], aimq, aimq, ALU.mult, r=kq, w=[("ARENA", "d2")])
        kb.tt("dve", den, den, arr[5], ALU.add, r=[("ARENA", "den"), ("ARENA", "d2")], w=[("ARENA", "den")])
        kb.op("dve", lambda e: e.reciprocal(out=den, in_=den), r=[("ARENA", "den")], w=[("ARENA", "den")])
        cre, cim = arr[0], arr[1]
        kb.tt("dve", cre, nr, areq, ALU.mult, r=[("ARENA", "nr")] + kq + kA, w=[("ARENA", "cre")])
        kb.tt("dve", arr[5], ni, aimq, ALU.mult, r=[("ARENA", "ni")] + kq, w=[("ARENA", "d2")])
        kb.tt("dve", cre, cre, arr[5], ALU.add, r=[("ARENA", "cre"), ("ARENA", "d2")], w=[("ARENA", "cre")])
        kb.tt("dve", cre, cre, den, ALU.mult, r=[("ARENA", "cre"), ("ARENA", "den")], w=[("ARENA", "cre")])
        kb.tt("dve", cim, ni, areq, ALU.mult, r=[("ARENA", "ni")] + kq + kA, w=[("ARENA", "cim")])
        kb.tt("dve", arr[5], nr, aimq, ALU.mult, r=[("ARENA", "nr")] + kq, w=[("ARENA", "d2")])
        kb.tt("dve", cim, cim, arr[5], ALU.subtract, r=[("ARENA", "cim"), ("ARENA", "d2")], w=[("ARENA", "cim")])
        kb.tt("dve", cim, cim, den, ALU.mult, r=[("ARENA", "cim"), ("ARENA", "den")], w=[("ARENA", "cim")])
        bre_b, bim_b = wsr[4], wsr[5]
        kb.tt("dve", BRE[:], cre, bre_b, ALU.mult, r=[("ARENA", "cre"), ("WS", 1)], w=["BRE"])
        kb.tt("dve", arr[5], cim, bim_b, ALU.mult, r=[("ARENA", "cim"), ("WS", 1)], w=[("ARENA", "d2")])
        kb.tt("dve", BRE[:], BRE[:], arr[5], ALU.subtract, r=["BRE", ("ARENA", "d2")], w=["BRE"])
        kb.tt("dve", BIM[:], cre, bim_b, ALU.mult, r=[("ARENA", "cre"), ("WS", 1)], w=["BIM"])
        kb.tt("dve", arr[5], cim, bre_b, ALU.mult, r=[("ARENA", "cim"), ("WS", 1)], w=[("ARENA", "d2")])
        kb.tt("dve", BIM[:], BIM[:], arr[5], ALU.add, r=["BIM", ("ARENA", "d2")], w=["BIM"])
        kb.op("pool", lambda e: e.memset(STT_[:], 0.0), w=STk)
        kb.op("pool", lambda e: e.memset(HC[:], 0.0), w=HCk)
        kb.op("pool", lambda e: e.memset(PREV[:], 0.0), w=["PREV"])
        kb.op("pool", lambda e: e.memset(SHO[:], 0.0), w=["SHO"])
        kb.fence({"PR", "ARENA", "WS"}, "dve", lambda e: e.memset(SMALL[:, 0:1], 0.0))

        pc = [0]

        def store_piece(q_, outb, dest, rc, lo, hi, r_keys, w_key, stream):
            if not isinstance(dest, tuple):
                kb.dma(q_, dest[rc * 128:(rc + 1) * 128, lo:hi], outb[:, 0:hi - lo], r=r_keys, w=[w_key], stream=stream)
                return
            flat, tbl, ndk = dest
            for c0, (off, ncol) in tbl.items():
                a, b_ = max(lo, c0), min(hi, c0 + ncol)
                if a >= b_:
                    continue
                view = flat[off:off + 128 * ndk * ncol].rearrange("(p a b) -> p a b", p=128, a=ndk)
                kb.dma(q_, view[:, rc, a - c0:b_ - c0], outb[:, a - lo:b_ - lo], r=r_keys, w=[w_key + (c0,)], stream=stream)

        def precast(wd, wsb, R, C, name):
            npiece = -(-C // 3072)
            cw = C // npiece
            assert cw * npiece == C
            for rc in range(R // 128):
                for pi_ in range(npiece):
                    s_ = pc[0] % 2
                    i_ = pc[0]
                    pc[0] += 1
                    inb = AR_[:, s_ * 3072:s_ * 3072 + cw]
                    outb = WS[s_][:].bitcast(BF16)[:, 0:cw]
                    kin = ("ARENA", "pin", s_)
                    kout = [("WS", 2 * s_), ("WS", 2 * s_ + 1)]
                    kb.dma("sp" if i_ % 2 else "act", inb, wd[rc * 128:(rc + 1) * 128, pi_ * cw:(pi_ + 1) * cw], w=[kin], stream=("pi", s_))
                    kb.cp(("dve", "act", "pool")[i_ % 3], outb, inb, r=[kin], w=kout)
                    store_piece("act" if i_ % 2 else "sp", outb, wsb, rc, pi_ * cw, (pi_ + 1) * cw, kout, ("SCR", name, i_), ("cs", s_))

        precast(w_in, w_in_b, D, NIN, "w_in")
        kb.fence({"ARENA", "WS", "SCR"}, "dve", lambda e: e.memset(SMALL[:, 0:1], 0.0))

        pc_jobs = []
        for wd_, wb_, R_, C_, nm_ in ((w_out, w_out_b, D, D, "w_out"), (w1, w1_b, D, DFF, "w1"), (w3, w3_b, D, DFF, "w3"),
                                      (w2, w2_b, DFF, D, "w2"), (glu_w, glu_b_, 1024, 1024, "glu")):
            npiece_ = -(-C_ // 1536)
            cw_ = C_ // npiece_
            assert cw_ * npiece_ == C_
            for rc_ in range(R_ // 128):
                for pi_ in range(npiece_):
                    pc_jobs.append((wd_[rc_ * 128:(rc_ + 1) * 128, pi_ * cw_:(pi_ + 1) * cw_], (wb_, rc_, pi_ * cw_, (pi_ + 1) * cw_), cw_, nm_))
        pc_pos = [0]
        OFk_all = [("OF", i_) for i_ in range(16)]

        def pc_emit(n):
            PRfl = PR[:].rearrange("p a b c -> p (a b c)")
            OFfl = OF[:].rearrange("p a b -> p (a b)")
            for _ in range(n):
                if pc_pos[0] >= len(pc_jobs):
                    return
                src_, dst_, cw_, nm_ = pc_jobs[pc_pos[0]]
                i_ = pc_pos[0]
                pc_pos[0] += 1
                kb.dma("sp", PRfl[:, 0:cw_], src_, w=[("PR", "r")], stream=("pj", 0))
                kb.cp("pool", OFfl[:, 0:cw_], PRfl[:, 0:cw_], r=[("PR", "r")], w=OFk_all)
                store_piece("sp", OFfl[:, 0:cw_], dst_[0], dst_[1], dst_[2], dst_[3], OFk_all, ("SCR", nm_, "d", i_), ("pk", 0))

        wsn = [0]
        ost = [0]

        WSB = [WS[s_ // 2][:, (s_ % 2) * 2048:(s_ % 2 + 1) * 2048].bitcast(BF16) for s_ in range(4)]

        def wload(src_ap, shape_view):
            s = wsn[0] % 4
            wsn[0] += 1
            key = ("WS", s)
            kb.dma("sp", shape_view(WSB[s]), src_ap, r=[("SCR", "x")], w=[key], stream=("ws", s))
            return shape_view(WSB[s]), key

        def colblock(wblk, nrow_chunks, c0, ncols):
            flat, tbl, ndk = wblk
            off, ncol_ = tbl[c0]
            assert ncol_ == ncols and ndk == nrow_chunks, (c0, ncols, ncol_)
            src = flat[off:off + 128 * ndk * ncols].rearrange("(p a b) -> p a b", p=128, a=ndk)
            return wload(src, lambda t: t[:, 0:nrow_chunks * ncols].rearrange("p (a b) -> p a b", a=nrow_chunks))

        def rowblock(wdram, r0, nchunks):
            src = wdram[r0:r0 + nchunks * 128, :].rearrange("(a p) c -> p a c", p=128)
            return wload(src, lambda t: t[:, 0:nchunks * D].rearrange("p (a b) -> p a b", a=nchunks))

        def layer_norm_tm(XT, rows_g, rows_b, keyx):
            for c in range(4):
                kb.op("dve", lambda e, c=c, src_=XT[:, c * 512:(c + 1) * 512]: e.bn_stats(out=STATS[:, c, :], in_=src_), r=[keyx], w=["STATS"])
            mv = SMALL[:, 2:4]; rs = SMALL[:, 4:5]
            kb.op("dve", lambda e: e.bn_aggr(out=mv, in_=STATS[:].rearrange("p a b -> p (a b)")), r=["STATS"], w=["mv"])
            kb.ts("dve", rs, mv[:, 1:2], LN_EPS, ALU.add, r=["mv"], w=["rs"])
            rsqrt_(rs, rs, ["rs"], ["rs"])
            kb.ts("dve", XT[:], XT[:], mv[:, 0:1], ALU.subtract, rs, ALU.mult, r=[keyx, "mv", "rs"], w=[keyx])
            kb.dma("sp", ROWS[:, 0, :], rows_d[rows_g:rows_g + 1, :].broadcast_to([128, D]), w=[("ROWS", 0)], stream=("rw", 0))
            kb.tt("dve", XT[:], XT[:], ROWS[:, 0, :], ALU.mult, r=[keyx, ("ROWS", 0)], w=[keyx])
            kb.dma("sp", ROWS[:, 0, :], rows_d[rows_b:rows_b + 1, :].broadcast_to([128, D]), w=[("ROWS", 0)], stream=("rw", 0))
            kb.tt("pool", XT[:], XT[:], ROWS[:, 0, :], ALU.add, r=[keyx, ("ROWS", 0)], w=[keyx])

        def to_fm(XT, keyx):
            for g4 in range(4):
                b = nb()
                for j in range(4):
                    dk = g4 * 4 + j
                    kb.tr(bank(b)[:, j * 128:(j + 1) * 128], XT[:, dk * 128:(dk + 1) * 128], ident, r=[keyx, "CONST"], w=[("ps", b)])
                kb.cp("act" if g4 % 2 else "dve", XF[:, g4 * 4:(g4 + 1) * 4, :],
                      bank(b).rearrange("p (a b) -> p a b", a=4), r=[("ps", b)], w=[("XF", g4)])

        XFk = [("XF", g) for g in range(4)]
        STk = [("ST", h_) for h_ in range(8)]

        prefetched = set()

        def prefetch_x(d):
            xsrc_, tix_ = d[0], d[1]
            XTn = XTS[tix_ % 2]
            kxn = "XT%d" % (tix_ % 2)
            kb.dma("sp", XTn[:], xsrc_, w=[kxn], stream=("x", tix_ % 2))
            layer_norm_tm(XTn, 0, 1, kxn)
            prefetched.add(tix_)

        def tile(xsrc, tix, full, seqs, yout, first_own, last_pre=False, nxt=None):
            XT = XTS[tix % 2]
            kx = "XT%d" % (tix % 2)
            sample = seqs is not None
            if STOP < 1:
                return
            kb.fence({"PR", "ARENA", "ARENA2", "WS", "WSS"}, "dve", lambda e: e.memset(SMALL[:, 0:1], 0.0))
            if first_own or (sample and pc_pos[0] < len(pc_jobs)):
                pc_emit(len(pc_jobs))
                kb.fence({"SCR", "PR"}, "dve", lambda e: e.memset(SMALL[:, 0:1], 0.0))
            if first_own:
                kb.ts("dve", STT_[:], STT_[:], FLAG[:], ALU.mult, r=STk + ["FLAG"], w=STk)
                kb.ts("dve", HC[:], HC[:], FLAG[:], ALU.mult, r=HCk + ["FLAG"], w=HCk)
                kb.ts("dve", PREV[:], PREV[:], FLAG[:], ALU.mult, r=["PREV", "FLAG"], w=["PREV"])
            if tix not in prefetched:
                prefetch_x((xsrc, tix))
            if sample:
                csv = WS[0][0:2, 0:3456]
                kb.dma("sp", csv, cshift[seqs[0]:seqs[0] + 2, :], w=[("WS", 0), ("WS", 1)], stream=("ws", 0))
                csk = ("WS", 0)
                b = nb()
                for fc in range(27):
                    kb.tr(bank(b)[:, fc * 2:fc * 2 + 2], csv[:, fc * 128:(fc + 1) * 128], ident[0:2, 0:2],
                          r=[("WS", 0), ("WS", 1), "CONST"], w=[("ps", b)])
                kb.cp("dve", SH0[:], bank(b)[:, 0:54].rearrange("p (a b) -> p a b", a=27), r=[("ps", b)], w=["SH0"])
                kb.cp("dve", PREV[:], SH0[:, :, 0], r=["SH0"], w=["PREV"])
            to_fm(XT, kx)
            if nxt is not None:
                prefetch_x(nxt)

            if STOP < 2:
                return
            RAWG = AR_[:, 0:8 * (T + 1)].rearrange("p (a b) -> p a b", a=8)
            DG_ = AR_[:, 1100:1100 + 8 * T].rearrange("p (a b) -> p a b", a=8)
            groups = [("k", 1024, 8, 8, 1), ("v", 2048, 8, 16, 2), ("l", 3072, 3, 24, None), ("u", 3360, 8, None, None)]
            fullp1 = full or last_pre
            if fullp1:
                groups = [("r", 0, 8, 0, 0)] + groups
            for gname, c0, nfc, mu0, pri in groups:
                kraw = ("ARENA", "raw")
                if gname != "u":
                    kb.cp("pool", RAWG[:, 0:nfc, 0], PREV[:, mu0:mu0 + nfc], r=["PREV"], w=[kraw])
                for blk in range((nfc + 1) // 2):
                    ncol = 256 if not (gname == "l" and blk == 1) else 32
                    if gname == "l" and blk == 1 and not fullp1:
                        continue
                    wv, wk = colblock(w_in_b, 16, c0 + blk * 256, ncol)
                    for j in range(2 if ncol == 256 else 1):
                        fc = blk * 2 + j
                        mrows = 128 if ncol == 256 else 32
                        if gname == "l" and not fullp1 and fc > 0:
                            continue
                        b = nb()
                        for dk in range(16):
                            kb.mm(bank(b)[0:mrows, 0:T], wv[:, dk, j * 128:j * 128 + mrows], XF[:, dk, :], start=(dk == 0), stop=(dk == 15),
                                  r=[wk] + XFk, w=[("ps", b)])
                        if gname == "u":
                            kb.cp("act", UF[:, fc, :], bank(b)[:, 0:T], r=[("ps", b)], w=[("UF", fc)])
                        else:
                            kb.cp("act", RAWG[0:mrows, fc, 1:T + 1], bank(b)[0:mrows, 0:T], r=[("ps", b)], w=[kraw])
                if gname == "u":
                    continue
                n_ = nfc if (fullp1 or gname != "l") else 1
                mu_b = COLS[:, mu0:mu0 + n_].unsqueeze(2).broadcast_to([128, n_, T])
                dst = PR[:, pri, :, PAD:PAD + T] if pri is not None else AR_[:, 2200:2200 + 3 * T].rearrange("p (a b) -> p a b", a=3)[:, 0:n_, :]
                kdst = ("PR", gname)
                kb.tt("dve", DG_[:, 0:n_, :], RAWG[:, 0:n_, 0:T], RAWG[:, 0:n_, 1:T + 1], ALU.subtract, r=[kraw], w=[("ARENA", "dg")])
                kb.tt("dve", DG_[:, 0:n_, :], DG_[:, 0:n_, :], mu_b, ALU.mult, r=[("ARENA", "dg"), "COLS"], w=[("ARENA", "dg")])
                kb.tt("dve", dst, DG_[:, 0:n_, :], RAWG[:, 0:n_, 1:T + 1], ALU.add, r=[("ARENA", "dg"), kraw], w=[kdst])
                if sample:
                    t0 = L
                    tmp = SMALL[:, 8:8 + n_]
                    kb.tt("dve", tmp, SH0[:, mu0:mu0 + n_, 1], RAWG[:, 0:n_, 1 + t0], ALU.subtract, r=["SH0", kraw], w=["sm8"])
                    kb.tt("dve", tmp, tmp, COLS[:, mu0:mu0 + n_], ALU.mult, r=["sm8", "COLS"], w=["sm8"])
                    kb.tt("dve", dst[:, :, t0], tmp, RAWG[:, 0:n_, 1 + t0], ALU.add, r=["sm8", kraw], w=[kdst])
                    for c in range(2):
                        kb.cp("pool", SHO[:, mu0:mu0 + n_, seqs[c]], RAWG[:, 0:n_, (c + 1) * L], r=[kraw], w=["SHO"])
                else:
                    kb.cp("pool", PREV[:, mu0:mu0 + n_], RAWG[:, 0:n_, T], r=[kraw], w=["PREV"])
                    if full:
                        kb.cp("pool", SHO[:, mu0:mu0 + n_, NSQ - 1], RAWG[:, 0:n_, T], r=[kraw], w=["SHO"])
            LT_ = AR_[:, 2200:2200 + 3 * T].rearrange("p (a b) -> p a b", a=3)
            kL = ("PR", "l")
            kb.act(LT_[0:64, 0, :], LT_[0:64, 0, :], AF.Tanh, r=[kL], w=[kL])
            if full:
                kb.act(LT_[:, 1, :], LT_[:, 1, :], AF.Sigmoid, r=[kL], w=[kL])
                kb.act(LT_[0:32, 2, :], LT_[0:32, 2, :], AF.Sigmoid, r=[kL], w=[kL])

            if STOP < 3:
                return
            YS = XF[:].rearrange("p a b -> p (a b)").bitcast(F32).rearrange("p (a b) -> p a b", a=8)
            def s5_gen(c, q, Wt_, Kq, smo):
                ps_re = bank(6); ps_im = bank(7)
                kre = [("ps", 6)]; kim = [("ps", 7)]
                W = [Wt_[:, i * 512:(i + 1) * 512] for i in range(6)]
                p3 = lambda x: x.rearrange("p (a b) -> p a b", a=8)
                sl8 = slice(q * 8, (q + 1) * 8)
                cosv = COS[:, sl8, :].rearrange("p a b -> p (a b)")
                sinv = SIN[:, sl8, :].rearrange("p a b -> p (a b)")
                rho0 = RHO0[:, sl8, :].rearrange("p a b -> p (a b)")
                kb.tt("dve", W[0], ps_re, cosv, ALU.mult, r=kre + ["SINCOS"], w=[Kq("w0")])
                kb.tt("dve", W[1], ps_im, sinv, ALU.mult, r=kim + ["SINCOS"], w=[Kq("w1")])
                kb.tt("dve", W[2], ps_im, cosv, ALU.mult, r=kim + ["SINCOS"], w=[Kq("w2")])
                kb.tt("dve", W[3], ps_re, sinv, ALU.mult, r=kre + ["SINCOS"], w=[Kq("w3")])
                yield
                kb.tt("pool", W[0], W[0], W[1], ALU.add, r=[Kq("w0"), Kq("w1")], w=[Kq("w0")])
                kb.tt("pool", W[2], W[2], W[3], ALU.subtract, r=[Kq("w2"), Kq("w3")], w=[Kq("w2")])
                for ri, Wt, kk_ in ((0, W[0], "w0"), (1, W[2], "w2")):
                    tmp = SMALL[:, smo + 8 * ri:smo + 8 + 8 * ri]
                    kb.tt("pool", tmp, HC[:, ri, sl8], RHO[:, sl8], ALU.mult, r=[("HC", q), "RHO"], w=[("smh", smo, ri)])
                    kb.tt("pool", p3(Wt)[:, :, 0], p3(Wt)[:, :, 0], tmp, ALU.add, r=[("smh", smo, ri), Kq(kk_)], w=[Kq(kk_)])
                yield
                kb.op("dve", lambda e, o=W[1], d=W[0], r0=rho0: e.tensor_tensor_scan(out=o, data0=r0, data1=d, initial=0.0, op0=ALU.mult, op1=ALU.add),
                      r=[Kq("w0"), "RHO0"], w=[Kq("w1")])
                kb.op("dve", lambda e, o=W[3], d=W[2], r0=rho0: e.tensor_tensor_scan(out=o, data0=r0, data1=d, initial=0.0, op0=ALU.mult, op1=ALU.add),
                      r=[Kq("w2"), "RHO0"], w=[Kq("w3")])
                yield
                gre, gim = W[1], W[3]
                if not full:
                    cl = COS[:, sl8, L - 1]; sl_ = SIN[:, sl8, L - 1]
                    gr = p3(gre)[:, :, L - 1]; gi_l = p3(gim)[:, :, L - 1]
                    t_ = [SMALL[:, smo + 16 + 8 * i:smo + 24 + 8 * i] for i in range(2)]
                    kb.tt("pool", t_[0], gr, cl, ALU.mult, r=[Kq("w1"), "SINCOS"], w=[("smt", smo, 0)])
                    kb.tt("pool", t_[1], gi_l, sl_, ALU.mult, r=[Kq("w3"), "SINCOS"], w=[("smt", smo, 1)])
                    kb.tt("pool", HC[:, 0, sl8], t_[0], t_[1], ALU.subtract, r=[("smt", smo, 0), ("smt", smo, 1)], w=[("HC", q)])
                    kb.tt("pool", t_[0], gi_l, cl, ALU.mult, r=[Kq("w3"), "SINCOS", ("HC", q)], w=[("smt", smo, 0)])
                    kb.tt("pool", t_[1], gr, sl_, ALU.mult, r=[Kq("w1"), "SINCOS", ("HC", q)], w=[("smt", smo, 1)])
                    kb.tt("pool", HC[:, 1, sl8], t_[0], t_[1], ALU.add, r=[("smt", smo, 0), ("smt", smo, 1)], w=[("HC", q)])
                    return
                kb.tt("dve", W[0], gre, cosv, ALU.mult, r=[Kq("w1"), "SINCOS"], w=[Kq("w0")])
                kb.tt("pool", W[2], gim, sinv, ALU.mult, r=[Kq("w3"), "SINCOS"], w=[Kq("w2")])
                kb.tt("dve", W[4], W[0], W[2], ALU.subtract, r=[Kq("w0"), Kq("w2")], w=[Kq("w4")])
                yield
                kb.tt("pool", W[0], gim, cosv, ALU.mult, r=[Kq("w3"), "SINCOS"], w=[Kq("w0")])
                kb.tt("dve", W[2], gre, sinv, ALU.mult, r=[Kq("w1"), "SINCOS"], w=[Kq("w2")])
                kb.tt("pool", W[5], W[0], W[2], ALU.add, r=[Kq("w0"), Kq("w2")], w=[Kq("w5")])
                kb.cp("pool", HC[:, 0, sl8], p3(W[4])[:, :, L - 1], r=[Kq("w4")], w=[("HC", q)])
                kb.cp("pool", HC[:, 1, sl8], p3(W[5])[:, :, L - 1], r=[Kq("w5")], w=[("HC", q)])
                yield
                b = 6
                for fc in range(8):
                    pi_ = q * 8 + fc
                    kb.mm(bank(b)[0:32, fc * 64:(fc + 1) * 64], CRE[:, pi_ * 32:(pi_ + 1) * 32], p3(W[4])[:, fc, :], start=True, stop=False,
                          r=["CRE", Kq("w4")], w=[("ps", b)])
                    kb.mm(bank(b)[0:32, fc * 64:(fc + 1) * 64], NCIM[:, pi_ * 32:(pi_ + 1) * 32], p3(W[5])[:, fc, :], start=False, stop=True,
                          r=["NCIM", Kq("w5")], w=[("ps", b)])
                kb.cp("dve", YS[32 * q:32 * q + 32, :, c * L:(c + 1) * L], bank(b)[0:32, :].rearrange("p (a b) -> p a b", a=8),
                      r=[("ps", b)], w=XFk)


            def s5_tile_gen():
                kb.fence({"WS", "WSS"}, "dve", lambda e: e.memset(SMALL[:, 0:1], 0.0))
                for c in range(NCH):
                    if sample:
                        kb.dma("sp", HC[:], s5st[seqs[c]].rearrange("a p b -> p a b"), w=HCk, stream=("hc", 0))
                    for q in range(4):
                        kb.mm(bank(6)[:, 0:128], ident, ident, r=["CONST"], w=[("ps", 6)])
                        for fc in range(8):
                            for ri, Bt in ((0, BRE), (1, BIM)):
                                kb.mm(bank(6 + ri)[:, fc * 64:(fc + 1) * 64], Bt[32 * q:32 * q + 32, fc * 128:(fc + 1) * 128],
                                      UF[32 * q:32 * q + 32, fc, c * L:(c + 1) * L], r=["BRE", "BIM", ("UF", fc)], w=[("ps", 6 + ri)], tile_position=(32 * q, 0))
                        yield
                        yield from s5_gen(c, q, WS[0], (lambda n: ("WSS", n)), 16)
                    if sample:
                        kb.dma("act", s5_o[seqs[c]].rearrange("a p b -> p a b"), HC[:], r=HCk, stream=("out", "s5"))

            kb.fence({"ARENA"}, "dve", lambda e: e.memset(SMALL[:, 0:1], 0.0))
            NIT = 2 * NCH

            def make_scratch(arena, pref, a_base, regions):
                regs = [list(r_) for r_ in regions]
                def carve(n):
                    for r_ in regs:
                        if r_[0] + n <= r_[1]:
                            v = arena[:, r_[0]:r_[0] + n]
                            r_[0] += n
                            return v
                    raise AssertionError("scratch overflow")
                S = {}
                S["A"] = lambda i: arena[:, a_base + i * T:a_base + (i + 1) * T]
                v3_ = lambda ap, n: ap.rearrange("p (a b) -> p a b", a=n)
                for nm in ("BK", "ARt", "BKs", "DGm"):
                    S[nm] = v3_(carve(NCH * 128), NCH)
                S["Mm"] = v3_(carve(NIT * 128), NIT)
                S["NTp"] = [v3_(carve(NIT * 64), NIT) for _ in range(2)]
                S["PQ"] = [v3_(carve(NIT * 128), NIT) for _ in range(2)]
                for nm in ("XBs", "BKT", "UV"):
                    S[nm] = v3_(carve(NIT * 64), NIT)
                S["Xs"] = carve(128)
                S["K"] = lambda n: (pref, n)
                return S

            SCR = [make_scratch(AR_, "ARENA", 2600, [(0, 2200), (2600 + 13 * T, 6144)]),
                   make_scratch(AR2, "ARENA2", 0, [(13 * T, 5632)])]

            def hp_gen(hp, S):
                A = S["A"]; K = S["K"]
                BK, ARt, BKs, DGm, Mm, NTp, PQ, XBs, BKT, UV, Xs = (S[n_] for n_ in ("BK", "ARt", "BKs", "DGm", "Mm", "NTp", "PQ", "XBs", "BKT", "UV", "Xs"))
                Tm = PQ[0][:, :, 64:128]
                hc = slice(hp * 128, (hp + 1) * 128)
                r_ = PR[:, 0, hp, PAD:PAD + T]; k_ = PR[:, 1, hp, PAD:PAD + T]; v_ = PR[:, 2, hp, PAD:PAD + T]
                sg, a_, kk, kp, bb, cs, gm, gi, gp, tq = A(0), A(1), A(2), A(3), A(4), A(5), A(6), A(7), A(8), A(9)
                gg, bon, yy = A(10), A(11), A(12)
                b = nb()
                kb.mm(bank(b)[:, 0:T], LORA[0:64, 0, hc], LT_[0:64, 0, :], r=["LORA", kL], w=[("ps", b)])
                kb.act(sg, bank(b)[:, 0:T], AF.Sigmoid, r=[("ps", b), "COLS"], w=[K("sg")], bias=col(CO_W0, hp))
                b = nb()
                kb.mm(bank(b)[:, 0:T], LORA[64:128, 0, hc], LT_[64:128, 0, :], r=["LORA", kL], w=[("ps", b)])
                kb.act(a_, bank(b)[:, 0:T], AF.Sigmoid, r=[("ps", b), "COLS"], w=[K("a")], bias=col(CO_A0, hp))
                yield
                if full:
                    b = nb()
                    kb.mm(bank(b)[:, 0:T], LORA[:, 1, hc], LT_[:, 1, :], start=True, stop=False, r=["LORA", kL], w=[("ps", b)])
                    kb.mm(bank(b)[:, 0:T], LORA[:, 2, hc], LT_[:, 2, :], start=False, stop=True, r=["LORA", kL], w=[("ps", b)])
                    kb.cp("act", gg, bank(b)[:, 0:T], r=[("ps", b)], w=[K("gg")])
                kb.ts("dve", kk, k_, col(CO_KK, hp), ALU.mult, r=[("PR", "k"), "COLS"], w=[K("kk")])
                kb.tt("pool", tq, kk, kk, ALU.mult, r=[K("kk")], w=[K("tq")])
                b = nb()
                kb.mm(bank(b)[:, 0:T], bones, tq, r=["CONST", K("tq")], w=[("ps", b)])
                kb.ts("dve", tq, bank(b)[:, 0:T], 1e-24, ALU.max, r=[("ps", b)], w=[K("tq")])
                rsqrt_(tq, tq, [K("tq")], [K("tq")])
                kb.tt("dve", kk, kk, tq, ALU.mult, r=[K("kk"), K("tq")], w=[K("kk")])
                yield
                kb.ts("dve", kp, a_, col(CO_KA, hp), ALU.mult, col(CO_OMKA, hp), ALU.add, r=[K("a"), "COLS"], w=[K("kp")])
                kb.tt("dve", kp, kp, k_, ALU.mult, r=[K("kp"), ("PR", "k")], w=[K("kp")])
                kb.tt("pool", bb, kk, a_, ALU.mult, r=[K("kk"), K("a")], w=[K("bb")])
                kb.op("dve", lambda e, cs=cs, sg=sg: e.tensor_tensor_scan(out=cs, data0=CM, data1=sg, initial=0.0, op0=ALU.mult, op1=ALU.add),
                      r=["CONST", K("sg")], w=[K("cs")])
                kb.act(gm, cs, AF.Exp, r=[K("cs")], w=[K("gm")], scale=-C0)
                kb.act(gi, cs, AF.Exp, r=[K("cs")], w=[K("gi")], scale=C0)
                kb.tt("pool", gp, cs, sg, ALU.subtract, r=[K("cs"), K("sg")], w=[K("gp")])
                kb.act(gp, gp, AF.Exp, r=[K("gp")], w=[K("gp")], scale=-C0)
                yield
                v4 = lambda x: x.rearrange("p (a b) -> p a b", a=NCH)
                kb.tt("dve", BK[:, :, 0:64], v4(bb), v4(gi), ALU.mult, r=[K("bb"), K("gi")], w=[K("BK")])
                kb.tt("dve", BK[:, :, 64:128], v4(kp), v4(gi), ALU.mult, r=[K("kp"), K("gi")], w=[K("BK")])
                kb.stt(ARt[:, :, 0:64], v4(kk), -1.0, v4(gp), ALU.mult, ALU.mult, r=[K("kk"), K("gp")], w=[K("AR")])
                if full:
                    kb.tt("dve", ARt[:, :, 64:128], v4(r_), v4(gm), ALU.mult, r=[("PR", "r"), K("gm")], w=[K("AR")])
                    kb.stt(tq, r_, col(CO_RK, hp), kp, ALU.mult, ALU.mult, r=[("PR", "r"), K("kp"), "COLS", K("tq")], w=[K("tq")])
                    b = nb()
                    kb.mm(bank(b)[:, 0:T], bones, tq, r=["CONST", K("tq")], w=[("ps", b)])
                    kb.tt("dve", bon, bank(b)[:, 0:T], v_, ALU.mult, r=[("ps", b), ("PR", "v")], w=[K("bon")])
                else:
                    kb.op("pool", lambda e: e.memset(ARt[:, :, 64:128], 0.0), w=[K("AR")])
                gL = v4(gm)[:, :, L - 1:L]
                kb.tt("dve", BKs[:], BK[:], gL.broadcast_to([128, NCH, 128]), ALU.mult, r=[K("BK"), K("gm")], w=[K("BKs")])
                kb.tt("pool", DGm[:], ident.unsqueeze(1).broadcast_to([128, NCH, 128]), gL.broadcast_to([128, NCH, 128]), ALU.mult,
                      r=["CONST", K("gm")], w=[K("DG")])
                yield
                if SUB < 1:
                    return
                pr_ = lambda par: slice(par * 64, par * 64 + 64)
                IT = lambda par, c: par * NCH + c
                v3 = lambda ap, n: ap.rearrange("p (a b) -> p a b", a=n)
                bp = [nb(), nb()]
                for par in range(2):
                    for c in range(NCH):
                        kb.mm(bank(bp[par])[:, c * 128:(c + 1) * 128], BK[pr_(par), c, :], ARt[pr_(par), c, :],
                              r=[K("BK"), K("AR")], w=[("ps", bp[par])])
                for par in range(2):
                    kb.tt("dve", Mm[:, par * NCH:(par + 1) * NCH, :], v3(bank(bp[par])[:, 0:NCH * 128], NCH),
                          MASK.unsqueeze(1).broadcast_to([128, NCH, 128]), ALU.mult, r=[("ps", bp[par]), "CONST"], w=[K("Mm")])
                bp = [nb(), nb()]
                for par in range(2):
                    for c in range(NCH):
                        kb.mm(bank(bp[par])[0:64, c * 64:(c + 1) * 64], ARt[pr_(par), c, 0:64], BK[pr_(par), c, 0:64], r=[K("AR"), K("BK")], w=[("ps", bp[par])])
                for par in range(2):
                    kb.tt("dve", NTp[0][0:64, par * NCH:(par + 1) * NCH, :], v3(bank(bp[par])[0:64, 0:NCH * 64], NCH),
                          LT.unsqueeze(1).broadcast_to([64, NCH, 64]), ALU.mult, r=[("ps", bp[par]), "CONST"], w=[K(("NT", 0))])
                kb.cp("pool", PQ[0][0:64, :, 0:64], Mm[0:64, :, 0:64], r=[K("Mm")], w=[K(("PQ", 0))])
                kb.cp("pool", PQ[0][0:64, :, 64:128], ident[0:64, 0:64].unsqueeze(1).broadcast_to([64, NIT, 64]), r=["CONST", K(("PQ", 0))], w=[K(("PQ", 0))])
                yield
                if SUB < 2:
                    return
                bp = [nb(), nb()]
                for par in range(2):
                    for c in range(NCH):
                        kb.tr(bank(bp[par])[:, c * 64:(c + 1) * 64], PR[pr_(par), 2, hp, PAD + c * L - 64:PAD + c * L + 64], ident[pr_(par), pr_(par)],
                              r=[("PR", "v"), "CONST"], w=[("ps", bp[par])])
                for par in range(2):
                    kb.cp("act", UV[64:128, par * NCH:(par + 1) * NCH, :], v3(bank(bp[par])[64:128, 0:NCH * 64], NCH), r=[("ps", bp[par])], w=[K("UVv")])
                bp = [nb(), nb()]
                for par in range(2):
                    for c in range(NCH):
                        kb.tr(bank(bp[par])[:, c * 64:(c + 1) * 64], BKs[pr_(par), c, :], ident[pr_(par), pr_(par)],
                              r=[K("BKs"), "CONST"], w=[("ps", bp[par])])
                for par in range(2):
                    kb.cp("act", BKT[:, par * NCH:(par + 1) * NCH, :], v3(bank(bp[par])[:, 0:NCH * 64], NCH), r=[("ps", bp[par])], w=[K("BKT")])
                b = nb()
                for it in range(NIT):
                    kb.mm(bank(b)[0:64, it * 64:(it + 1) * 64], Mm[64:128, it, 0:64], UV[64:128, it, :], r=[K("Mm"), K("UVv")], w=[("ps", b)])
                kb.cp("act", XBs[0:64, :, :], v3(bank(b)[0:64, 0:NIT * 64], NIT), r=[("ps", b)], w=[K("XBs")])
                yield
                if SUB < 3:
                    return
                for lev in range(6):
                    cur, nxt = lev % 2, (lev + 1) % 2
                    last = lev == 5
                    b = nb()
                    for it in range(NIT):
                        kb.mm(bank(b)[0:64, it * 128:(it + 1) * 128], NTp[cur][:, it, :], PQ[cur][:, it, :], r=[K(("NT", cur)), K(("PQ", cur)), K("zpad")], w=[("ps", b)])
                    if not last:
                        b2 = nb()
                        for it in range(NIT):
                            kb.mm(bank(b2)[0:64, it * 64:(it + 1) * 64], PQ[cur][:, it, 0:64], NTp[cur][:, it, :], r=[K(("NT", cur)), K(("PQ", cur)), K("zpad")], w=[("ps", b2)])
                    pa = v3(bank(b)[0:64, 0:NIT * 128], NIT)
                    kb.tt("dve", PQ[nxt][0:64, :, 64:128], PQ[cur][0:64, :, 64:128], pa[:, :, 64:128], ALU.add, r=[("ps", b), K(("PQ", cur))], w=[K(("PQ", nxt))])
                    if not last:
                        kb.cp("act", PQ[nxt][0:64, :, 0:64], pa[:, :, 0:64], r=[("ps", b), K(("PQ", nxt))], w=[K(("PQ", nxt))])
                        kb.cp("act", NTp[nxt][0:64, :, :], v3(bank(b2)[0:64, 0:NIT * 64], NIT), r=[("ps", b2)], w=[K(("NT", nxt))])
                    yield
                if SUB < 4:
                    return
                for c in range(NCH):
                    if sample:
                        kb.dma("sp", STT_[:, hp, :], swkv[seqs[c], :, hp * 64:(hp + 1) * 64], w=[("ST", hp)], stream=("st", hp))
                    bp = [nb(), nb()]
                    for par in range(2):
                        kb.mm(bank(bp[par])[0:64, 0:64], ARt[pr_(par), c, 0:64], STT_[pr_(par), hp, :], r=[K("AR"), ("ST", hp)], w=[("ps", bp[par])])
                    for par in range(2):
                        kb.tt("dve", Xs[0:64, par * 64:(par + 1) * 64], bank(bp[par])[0:64, 0:64], XBs[0:64, IT(par, c), :], ALU.add,
                              r=[("ps", bp[par]), K("XBs")], w=[K(("Xs", par))])
                    yield
                    b = nb()
                    for par in range(2):
                        kb.mm(bank(b)[0:64, par * 64:(par + 1) * 64], Tm[0:64, IT(par, c), :], Xs[0:64, par * 64:(par + 1) * 64], r=[K(("PQ", 0)), K(("Xs", par))], w=[("ps", b)])
                    for par in range(2):
                        kb.cp("act", UV[0:64, IT(par, c), :], bank(b)[0:64, par * 64:(par + 1) * 64], r=[("ps", b)], w=[K("UVu")])
                    yield
                    if full:
                        bp = [nb(), nb()]
                        for par in range(2):
                            kb.mm(bank(bp[par])[0:64, 0:64], STT_[pr_(par), hp, :], ARt[pr_(par), c, 64:128], start=True, stop=False,
                                  r=[("ST", hp), K("AR")], w=[("ps", bp[par])])
                        for par in range(2):
                            kb.mm(bank(bp[par])[0:64, 0:64], UV[:, IT(par, c), :], Mm[:, IT(par, c), 64:128], start=False, stop=True,
                                  r=[K("UVu"), K("UVv"), K("Mm")], w=[("ps", bp[par])])
                        kb.cp("act", yy[0:64, c * L:(c + 1) * L], bank(bp[0])[0:64, 0:64], r=[("ps", bp[0])], w=[K("yy")])
                        kb.cp("dve", yy[64:128, c * L:(c + 1) * L], bank(bp[1])[0:64, 0:64], r=[("ps", bp[1])], w=[K("yy")])
                    bp = [nb(), nb()]
                    for par in range(2):
                        kb.mm(bank(bp[par])[0:64, 0:64], DGm[pr_(par), c, pr_(par)], STT_[pr_(par), hp, :], start=True, stop=False,
                              r=[K("DG"), ("ST", hp)], w=[("ps", bp[par])])
                    for par in range(2):
                        kb.mm(bank(bp[par])[0:64, 0:64], BKT[:, IT(par, c), :], UV[:, IT(par, c), :], start=False, stop=True,
                              r=[K("BKT"), K("UVu"), K("UVv")], w=[("ps", bp[par])])
                    kb.cp("act", STT_[0:64, hp, :], bank(bp[0])[0:64, 0:64], r=[("ps", bp[0])], w=[("ST", hp)])
                    kb.cp("dve", STT_[64:128, hp, :], bank(bp[1])[0:64, 0:64], r=[("ps", bp[1])], w=[("ST", hp)])
                    yield
                    if sample:
                        kb.dma("act", wkv_o[seqs[c], :, hp * 64:(hp + 1) * 64], STT_[:, hp, :], r=[("ST", hp)], stream=("out", "wkv", hp))
                if full and SUB >= 5:
                    b = nb()
                    kb.mm(bank(b)[:, 0:T], bones, yy, r=["CONST", K("yy")], w=[("ps", b)])
                    kb.stt(yy, bank(b)[:, 0:T], -1.0 / 64, yy, ALU.mult, ALU.add, r=[("ps", b), K("yy")], w=[K("yy")])
                    kb.tt("pool", tq, yy, yy, ALU.mult, r=[K("yy")], w=[K("tq")])
                    yield
                    b = nb()
                    kb.mm(bank(b)[:, 0:T], bones, tq, r=["CONST", K("tq")], w=[("ps", b)])
                    kb.ts("dve", tq, bank(b)[:, 0:T], 1.0 / 64, ALU.mult, GN_EPS, ALU.add, r=[("ps", b)], w=[K("tq")])
                    rsqrt_(tq, tq, [K("tq")], [K("tq")])
                    kb.tt("dve", yy, yy, tq, ALU.mult, r=[K("yy"), K("tq")], w=[K("yy")])
                    kb.ts("dve", yy, yy, col(CO_GG, hp), ALU.mult, col(CO_GB, hp), ALU.add, r=[K("yy"), "COLS"], w=[K("yy")])
                    kb.tt("pool", yy, yy, bon, ALU.add, r=[K("yy"), K("bon")], w=[K("yy")])
                    kb.tt("pool", OF[:, hp, :], yy, gg, ALU.mult, r=[K("yy"), K("gg")], w=[("OF", hp)])
            for S_ in SCR:
                for t_ in S_["NTp"] + S_["PQ"]:
                    kb.op("pool", lambda e, t_=t_: e.memset(t_[64:128, :, :], 0.0), w=[S_["K"]("zpad")])
            s5g = s5_tile_gen()
            pc_tick = [0]; pc_done = [0]
            pc_quota = -(-len(pc_jobs) // max(1, NPRE - 1)) if NPRE > 1 else 0
            s5_alive = [True]

            def s5_step():
                if s5_alive[0]:
                    try:
                        next(s5g)
                    except StopIteration:
                        s5_alive[0] = False

            for hp0 in range(0, 8, 2):
                gens = [hp_gen(hp0, SCR[0]), hp_gen(hp0 + 1, SCR[1])]
                alive = [True, True]
                while any(alive):
                    for gi_ in range(2):
                        if alive[gi_]:
                            try:
                                next(gens[gi_])
                            except StopIteration:
                                alive[gi_] = False
                    s5_step()
                    pc_tick[0] += 1
                    if (not full) and (not last_pre) and pc_tick[0] % 4 == 0 and pc_done[0] < pc_quota:
                        pc_emit(1)
                        pc_done[0] += 1
            while s5_alive[0]:
                s5_step()

            if not sample and full:
                pass
            if STOP < 4:
                return
            K = lambda n: ("ARENA", n)
            if not full or STOP < 5:
                return
            Z = AR_[:, 0:1024].rearrange("p (a b) -> p a b", a=8); Z2 = AR_[:, 1024:2048].rearrange("p (a b) -> p a b", a=8)
            Z3 = AR_[:, 2048:3072].rearrange("p (a b) -> p a b", a=8)
            kb.fence({"ARENA", "ARENA2", "WS", "WSS"}, "dve", lambda e: e.memset(SMALL[:, 0:1], 0.0))
            sd_b = COLS[:, CO_SD:CO_SD + 8].unsqueeze(2).broadcast_to([128, 8, T])
            kb.tt("dve", Z[:], UF[:], sd_b, ALU.mult, r=[("UF", i) for i in range(8)] + ["COLS"], w=[K("z")])
            kb.tt("dve", Z[:], Z[:], YS, ALU.add, r=[K("z")] + XFk, w=[K("z")])
            kb.tt("pool", Z2[:], Z[:], Z[:], ALU.mult, r=[K("z")], w=[K("z2")])
            kb.ts("dve", Z2[:], Z2[:], 0.044715, ALU.mult, 1.0, ALU.add, r=[K("z2")], w=[K("z2")])
            kb.tt("dve", Z2[:], Z2[:], Z[:], ALU.mult, r=[K("z2"), K("z")], w=[K("z2")])
            kb.act(Z2[:], Z2[:], AF.Sigmoid, r=[K("z2")], w=[K("z2")], scale=2.0 * math.sqrt(2.0 / math.pi))
            Zb = AR_[:, 3072:3072 + 512].bitcast(BF16).rearrange("p (a b) -> p a b", a=8)
            kb.tt("dve", Zb, Z[:], Z2[:], ALU.mult, r=[K("z2"), K("z")], w=[K("zb")])
            for blk in range(4):
                wv, wk = colblock(glu_b_, 8, blk * 256, 256)
                for j in range(2):
                    oc = blk * 2 + j
                    b = nb()
                    for dk in range(8):
                        kb.mm(bank(b)[:, 0:T], wv[:, dk, j * 128:(j + 1) * 128], Zb[:, dk, :], start=(dk == 0), stop=(dk == 7), r=[wk, K("zb")], w=[("ps", b)])
                    kb.act(Z3[:, oc, :], bank(b)[:, 0:T], AF.Sigmoid, r=[("ps", b), "COLS"], w=[K("z3")], bias=col(CO_GLB, oc))
            kb.tt("dve", OF[:, 8:16, :], Zb, Z3[:], ALU.mult, r=[K("zb"), K("z3")], w=[("OF", 8)])
            if STOP < 6:
                return
            OFk = [("OF", i) for i in range(9)]
            for dk2 in range(8):
                wv, wk = rowblock(w_out_b, dk2 * 256, 2)
                for j in range(2):
                    dk = dk2 * 2 + j
                    for cb in range(4):
                        kb.mm(bank(cb), OF[:, dk, :], wv[:, j, cb * 512:(cb + 1) * 512], start=(dk == 0), stop=(dk == 15), r=[wk] + OFk, w=[("ps", cb)])
            for cb in range(4):
                kb.stt(XT[:, cb * 512:(cb + 1) * 512], XT[:, cb * 512:(cb + 1) * 512], ALPHA, bank(cb), ALU.mult, ALU.add, r=[kx, ("ps", cb)], w=[kx])
            layer_norm_tm(XT, 2, 3, kx)
            to_fm(XT, kx)
            if STOP < 7:
                return
            kb.fence({"PR", "ARENA", "ARENA2", "WS", "WSS"}, "dve", lambda e: e.memset(SMALL[:, 0:1], 0.0))
            GF = PR[:].rearrange("p a b c -> p (a b c)")[:, 0:11 * T // 2].bitcast(BF16).rearrange("p (a b) -> p a b", a=11)
            H1 = AR_[:, 0:T]
            for qt in range(4):
                for fl in range(11):
                    f = qt * 11 + fl
                    if fl % 2 == 0:
                        ncol = 256 if fl < 10 else 128
                        w1v, w1k = colblock(w1_b, 16, f * 128, ncol)
                        w3v, w3k = colblock(w3_b, 16, f * 128, ncol)
                    j = fl % 2
                    ba, bb_ = 4 + (fl % 2) * 2, 5 + (fl % 2) * 2
                    for dk in range(16):
                        kb.mm(bank(ba)[:, 0:T], w1v[:, dk, j * 128:(j + 1) * 128], XF[:, dk, :], start=(dk == 0), stop=(dk == 15), r=[w1k] + XFk, w=[("ps", ba)])
                    for dk in range(16):
                        kb.mm(bank(bb_)[:, 0:T], w3v[:, dk, j * 128:(j + 1) * 128], XF[:, dk, :], start=(dk == 0), stop=(dk == 15), r=[w3k] + XFk, w=[("ps", bb_)])
                    h1 = AR_[:, (fl % 2) * T:(fl % 2 + 1) * T]
                    kb.act(h1, bank(ba)[:, 0:T], AF.Silu, r=[("ps", ba)], w=[K(("h1", fl % 2))])
                    kb.tt("dve", GF[:, fl, :], h1, bank(bb_)[:, 0:T], ALU.mult, r=[K(("h1", fl % 2)), ("ps", bb_)], w=[("PR", ("gf", fl))])
                for fl2 in range(6):
                    nch = 2 if fl2 < 5 else 1
                    wv, wk = rowblock(w2_b, (qt * 11 + fl2 * 2) * 128, nch)
                    for j in range(nch):
                        fl = fl2 * 2 + j
                        f = qt * 11 + fl
                        for cb in range(4):
                            kb.mm(bank(cb), GF[:, fl, :], wv[:, j, cb * 512:(cb + 1) * 512], start=(f == 0), stop=(f == 43),
                                  r=[wk, ("PR", ("gf", fl))], w=[("ps", cb)])
            for cb in range(4):
                kb.stt(XT[:, cb * 512:(cb + 1) * 512], XT[:, cb * 512:(cb + 1) * 512], ALPHA, bank(cb), ALU.mult, ALU.add, r=[kx, ("ps", cb)], w=[kx])
            layer_norm_tm(XT, 4, 5, kx)
            kb.dma("sp", yout, XT[:], r=[kx], stream=("out", "y", tix % 2))

        descs = []
        for ti in range(NTP):
            full = ti >= NPRE
            descs.append((xp[ti * T:(ti + 1) * T, :], ti, full, None, yp[(ti - NPRE) * T:(ti - NPRE + 1) * T, :] if full else None, ti == NPRE, ti == NPRE - 1))
        for si in range(NSAMP_TILES):
            descs.append((xs[si * T:(si + 1) * T, :], NTP + si, True, [2 * si, 2 * si + 1], ys_o[si * T:(si + 1) * T, :], False, False))
        for ti in range(NTP):
            tile(*descs[ti], nxt=descs[ti + 1] if ti + 1 < len(descs) else None)
        for h_ in range(8):
            kb.dma("sp", wkv_o[NSQ - 1, :, h_ * 64:(h_ + 1) * 64], STT_[:, h_, :], r=[("ST", h_)], stream=("out", "wkv", h_))
        kb.dma("sp", s5_o[NSQ - 1].rearrange("a p b -> p a b"), HC[:], r=HCk, stream=("out", "s5"))
        for si in range(NSAMP_TILES):
            tile(*descs[NTP + si], nxt=descs[NTP + si + 1] if NTP + si + 1 < len(descs) else None)
        for sq in range(NSQ):
            b = nb()
            kb.tr(bank(b)[0:27, 0:128], SHO[:, :, sq], ident, r=["SHO", "CONST"], w=[("ps", b)])
            kb.cp("dve", ROWS[0:27, 0, sq * 128:(sq + 1) * 128], bank(b)[0:27, 0:128], r=[("ps", b)], w=[("ROWS", 0)])
        kb.dma("sp", shift_o.rearrange("s a b -> a s b"), ROWS[0:27, 0, 0:NSQ * 128].rearrange("p (s b) -> p s b", s=NSQ), r=[("ROWS", 0)], stream=("out", "sh"))
        kb.emit()
    return nc


def _fmcols(v, n):
    v = np.asarray(v, np.float32).reshape(-1)
    out = np.zeros(n * 128, np.float32)
    out[:v.size] = v
    return out.reshape(n, 128).T


def _consts():
    c = np.zeros((128, NCONST), np.float32)
    c[:, K_ID:K_ID + 128] = np.eye(128)
    bo = np.zeros((128, 128)); bo[:64, :64] = 1; bo[64:, 64:] = 1
    c[:, K_BO:K_BO + 128] = bo
    ma = np.triu(np.ones((64, 64)), 1); mr = np.triu(np.ones((64, 64)), 0)
    c[:, K_MASK:K_MASK + 128] = np.block([[ma, mr], [ma, mr]])
    c[0:64, K_LT:K_LT + 64] = np.tril(np.ones((64, 64)), -1)
    cm = np.ones(T); cm[::L] = 0
    c[:, K_CM:K_CM + T] = cm
    c[:, K_TAU:K_TAU + 64] = np.arange(1, 65)
    r0 = np.ones(64); r0[0] = 0
    c[:, K_R0:K_R0 + 64] = r0
    c[:, K_PI2] = math.pi / 2
    return c


def _pair_of(fc, q):
    return q * 8 + fc


def prep_shared(inp):
    f = lambda k: np.asarray(inp[k], np.float32)
    sh = {}
    sh["w_in"] = np.ascontiguousarray(f("w_in")[0]); sh["w_out"] = np.ascontiguousarray(f("w_out")[0])
    sh["w1"] = np.ascontiguousarray(f("ffn_w1")[0]); sh["w3"] = np.ascontiguousarray(f("ffn_w3")[0]); sh["w2"] = np.ascontiguousarray(f("ffn_w2")[0])
    sh["glu_w"] = np.ascontiguousarray(f("glu_w")[0])
    lora = np.zeros((3, 128, 1024), np.float32)
    lora[0, :64] = f("w_lora_up")[0]; lora[0, 64:] = f("a_lora_up")[0]
    lora[1] = f("g_lora_up")[0][:128]; lora[2, :32] = f("g_lora_up")[0][128:]
    sh["lora"] = lora
    cols = np.zeros((128, NCOL), np.float32)
    cols[:, CO_MU:CO_MU + 27] = _fmcols(f("mu_shift")[0], 27)
    for off, key in ((CO_W0, "w0"), (CO_A0, "a0"), (CO_KK, "k_k"), (CO_KA, "k_a"), (CO_RK, "r_k"), (CO_GG, "gn_g"), (CO_GB, "gn_b"), (CO_SD, "s5_d"), (CO_GLB, "glu_b")):
        cols[:, off:off + 8] = _fmcols(f(key)[0], 8)
    sh["cols"] = cols
    sh["consts"] = _consts()
    sh["rows"] = np.stack([f("ln_in_g"), f("ln_in_b"), f("ln1_g")[0], f("ln1_b")[0], f("ln2_g")[0], f("ln2_b")[0]]).astype(np.float32)
    a_re, a_im, ldt = f("s5_a_re")[0], f("s5_a_im")[0], f("s5_log_dt")[0]
    b_re, b_im, c_re, c_im = f("s5_b_re")[0], f("s5_b_im")[0], f("s5_c_re")[0], f("s5_c_im")[0]
    s5a = np.zeros((128, 3, 32), np.float32); s5q = np.zeros((128, 3, 8, 128), np.float32)
    bblk = np.zeros((128, 2, 8, 128), np.float32); cblk = np.zeros((128, 2, 32, 32), np.float32)
    for fc in range(8):
        for q in range(4):
            pi = _pair_of(fc, q)
            for g2 in range(2):
                g = 8 * fc + 2 * q + g2
                s5a[g2 * 64:(g2 + 1) * 64, 0, pi] = a_re[g]; s5a[g2 * 64:(g2 + 1) * 64, 1, pi] = a_im[g]; s5a[g2 * 64:(g2 + 1) * 64, 2, pi] = ldt[g]
                s5q[32 * q:32 * q + 32, 0, fc, g2 * 64:(g2 + 1) * 64] = a_re[g][None]
                s5q[32 * q:32 * q + 32, 1, fc, g2 * 64:(g2 + 1) * 64] = a_im[g][None]
                s5q[32 * q:32 * q + 32, 2, fc, g2 * 64:(g2 + 1) * 64] = ldt[g]
                bblk[32 * q + 16 * g2:32 * q + 16 * g2 + 16, 0, fc, g2 * 64:(g2 + 1) * 64] = b_re[g].T
                bblk[32 * q + 16 * g2:32 * q + 16 * g2 + 16, 1, fc, g2 * 64:(g2 + 1) * 64] = b_im[g].T
                cblk[g2 * 64:(g2 + 1) * 64, 0, pi, g2 * 16:(g2 + 1) * 16] = c_re[g].T
                cblk[g2 * 64:(g2 + 1) * 64, 1, pi, g2 * 16:(g2 + 1) * 16] = c_im[g].T
    sh["s5a"] = s5a.reshape(128, 96); sh["s5q"] = s5q.reshape(128, 3072)
    sh["bblk"] = bblk.reshape(128, 2048); sh["cblk"] = cblk.reshape(128, 2048)
    return sh


def s5_state_to_dev(re, im):
    out = np.zeros((2, 128, 32), np.float32)
    for fc in range(8):
        for q in range(4):
            for g2 in range(2):
                g = 8 * fc + 2 * q + g2
                out[0, g2 * 64:(g2 + 1) * 64, _pair_of(fc, q)] = re[g]
                out[1, g2 * 64:(g2 + 1) * 64, _pair_of(fc, q)] = im[g]
    return out


def s5_state_from_dev(d):
    re = np.zeros((64, 64), np.float32); im = np.zeros((64, 64), np.float32)
    for fc in range(8):
        for q in range(4):
            for g2 in range(2):
                g = 8 * fc + 2 * q + g2
                re[g] = d[0, g2 * 64:(g2 + 1) * 64, _pair_of(fc, q)]
                im[g] = d[1, g2 * 64:(g2 + 1) * 64, _pair_of(fc, q)]
    return re, im


def wkv_to_dev(s):
    return np.ascontiguousarray(s.reshape(8, 2, 64, 64).transpose(1, 3, 0, 2).reshape(128, 512))


def wkv_from_dev(d):
    return np.ascontiguousarray(d.reshape(2, 64, 8, 64).transpose(2, 0, 3, 1).reshape(16, 64, 64))


def run(inputs, n_cores, seq_len, n_samp_per_core):
    sh = prep_shared(inputs)
    half = seq_len // 2
    NH = half // T
    nst = n_samp_per_core // 2
    nc = build(NH, NH, nst)
    xp = np.asarray(inputs["x_prompt"], np.float32); xs = np.asarray(inputs["x_sample"], np.float32)
    cs = np.asarray(inputs["cache_shift"], np.float32)[0]; sw = np.asarray(inputs["state_wkv"], np.float32)[0]
    sre = np.asarray(inputs["state_s5_re"], np.float32)[0]; sim = np.asarray(inputs["state_s5_im"], np.float32)[0]
    in_maps = []
    for c in range(n_cores):
        s, h = c // 2, c % 2
        m = dict(sh)
        own = xp[s, h * half:(h + 1) * half]
        pre = xp[s, 0:half]
        m["xp"] = np.ascontiguousarray(np.concatenate([pre, own], 0))
        m["flag"] = np.full((128, 1), float(h), np.float32)
        sq = range(c * n_samp_per_core, (c + 1) * n_samp_per_core)
        m["xs"] = np.ascontiguousarray(xs[list(sq)].reshape(-1, D))
        csh = np.zeros((n_samp_per_core, 27 * 128), np.float32); csh[:, :NSH] = cs[list(sq), 0]
        m["cshift"] = csh
        m["swkv"] = np.stack([wkv_to_dev(sw[i]) for i in sq])
        m["s5st"] = np.stack([s5_state_to_dev(sre[i], sim[i]) for i in sq])
        in_maps.append(m)
    res = run_bass_kernel_spmd(nc, in_maps, core_ids=list(range(n_cores)))
    R = res.results
    B = n_cores // 2
    NS = n_cores * n_samp_per_core
    y_p = np.zeros((B, seq_len, D), np.float32); y_s = np.zeros((NS, 64, D), np.float32)
    sh_p = np.zeros((1, B, 1, NSH), np.float32); wkv_p = np.zeros((1, B, 16, 64, 64), np.float32)
    re_p = np.zeros((1, B, 64, 64), np.float32); im_p = np.zeros((1, B, 64, 64), np.float32)
    sh_s = np.zeros((1, NS, 1, NSH), np.float32); wkv_s = np.zeros((1, NS, 16, 64, 64), np.float32)
    re_s = np.zeros((1, NS, 64, 64), np.float32); im_s = np.zeros((1, NS, 64, 64), np.float32)
    for c in range(n_cores):
        s, h = c // 2, c % 2
        r = R[c]
        y_p[s, h * half:(h + 1) * half] = r["yp"]
        y_s[c * n_samp_per_core:(c + 1) * n_samp_per_core] = r["ys"].reshape(n_samp_per_core, 64, D)
        for i in range(n_samp_per_core):
            gi = c * n_samp_per_core + i
            sh_s[0, gi, 0] = r["shift_o"][i].reshape(-1)[:NSH]
            wkv_s[0, gi] = wkv_from_dev(r["wkv_o"][i])
            re_s[0, gi], im_s[0, gi] = s5_state_from_dev(r["s5_o"][i])
        if h == 1:
            i = n_samp_per_core
            sh_p[0, s, 0] = r["shift_o"][i].reshape(-1)[:NSH]
            wkv_p[0, s] = wkv_from_dev(r["wkv_o"][i])
            re_p[0, s], im_p[0, s] = s5_state_from_dev(r["s5_o"][i])
    return (y_p, y_s, sh_p, wkv_p, re_p, im_p, sh_s, wkv_s, re_s, im_s)


def kernel(**inputs):
    return run(inputs, 8, 4096, 4)
```

```python
import math
import contextlib
import numpy as np
import concourse.bass as bass
import concourse.mybir as mybir
from concourse.bass_utils import run_bass_kernel_spmd

F32 = mybir.dt.float32
I32 = mybir.dt.int32
BF16 = mybir.dt.bfloat16
AF = mybir.ActivationFunctionType
ALU = mybir.AluOpType

D = 2048
DFF = 5632
NSH = 3360
NIN = 4384
T = 128
L = 64
NCH = T // L
PAD = 64
ALPHA = 2.0 ** 0.25
LN_EPS = 1e-5
GN_EPS = 64e-5
C0 = math.exp(-0.5)


class Op:
    __slots__ = ("eng", "fn", "r", "w", "stream", "deps", "sig", "idx", "cnt")

    def __init__(self, eng, fn, r, w, stream):
        self.eng, self.fn, self.r, self.w, self.stream = eng, fn, r, w, stream
        self.deps = []
        self.sig = False
        self.cnt = 0


def _pref(k):
    return k[0] if isinstance(k, tuple) else k


class KB:
    ENGS = ("pe", "dve", "act", "pool", "sp")

    def __init__(self, nc):
        self.nc = nc
        self.ops = []
        self.last_w = {}
        self.readers = {}
        self.floor = {}
        self.streams = []

    def op(self, eng, fn, r=(), w=(), stream=None):
        o = Op(eng, fn, tuple(r), tuple(w), stream)
        o.idx = len(self.ops)
        deps = set()
        for k in o.r + o.w:
            p = self.last_w.get(k)
            if p is None:
                p = self.floor.get(_pref(k))
            if p is not None:
                deps.add(p)
        for k in o.w:
            for q in self.readers.get(k, ()):
                deps.add(q)
        deps.discard(o.idx)
        o.deps = sorted(deps)
        for k in o.w:
            self.last_w[k] = o.idx
            self.readers[k] = []
        for k in o.r:
            if k not in o.w:
                self.readers.setdefault(k, []).append(o.idx)
        if stream is not None and stream not in self.streams:
            self.streams.append(stream)
        self.ops.append(o)
        return o

    def fence(self, prefixes, eng, fn):
        keys = [k for k in set(self.last_w) | set(self.readers) if _pref(k) in prefixes]
        o = self.op(eng, fn, r=(), w=keys)
        for k in keys:
            self.last_w.pop(k, None)
            self.readers.pop(k, None)
        for p in prefixes:
            self.floor[p] = o.idx
        return o

    def mm(self, out, lhsT, rhs, start=True, stop=True, r=(), w=(), **kw):
        return self.op("pe", lambda e: e.matmul(out, lhsT, rhs, start=start, stop=stop, **kw), r, w)

    def tr(self, out, in_, ident, r=(), w=()):
        return self.op("pe", lambda e: e.transpose(out, in_, ident), r, w)

    def dma(self, q, out, in_, r=(), w=(), stream=None):
        return self.op(q, lambda e: e.dma_start(out=out, in_=in_), r, w, stream)

    def act(self, out, in_, func, r=(), w=(), **kw):
        return self.op("act", lambda e: e.activation(out=out, in_=in_, func=func, **kw), r, w)

    def tt(self, eng, out, a, b, op, r=(), w=()):
        return self.op(eng, lambda e: e.tensor_tensor(out=out, in0=a, in1=b, op=op), r, w)

    def ts(self, eng, out, a, s1, op0, s2=None, op1=None, r=(), w=()):
        if op1 is None:
            return self.op(eng, lambda e: e.tensor_scalar(out=out, in0=a, scalar1=s1, scalar2=None, op0=op0), r, w)
        return self.op(eng, lambda e: e.tensor_scalar(out=out, in0=a, scalar1=s1, scalar2=s2, op0=op0, op1=op1), r, w)

    def stt(self, out, a, s, b, op0, op1, r=(), w=()):
        return self.op("dve", lambda e: e.scalar_tensor_tensor(out=out, in0=a, scalar=s, in1=b, op0=op0, op1=op1), r, w)

    def cp(self, eng, out, in_, r=(), w=()):
        if eng == "act":
            return self.act(out, in_, AF.Copy, r, w)
        return self.op(eng, lambda e: e.tensor_copy(out=out, in_=in_), r, w)

    def emit(self):
        nc = self.nc
        ops = self.ops
        is_dma = lambda o: o.stream is not None

        def skip(p, o):
            return (not is_dma(p)) and (not is_dma(o)) and p.eng == "pe" and o.eng == "pe"

        for o in ops:
            for d in o.deps:
                p = ops[d]
                if not skip(p, o):
                    p.sig = True
        cnt = {e: 0 for e in self.ENGS}
        scnt = {s: 0 for s in self.streams}
        for o in ops:
            if is_dma(o):
                o.sig = True
                scnt[o.stream] += 16
                o.cnt = scnt[o.stream]
            elif o.sig:
                cnt[o.eng] += 1
                o.cnt = cnt[o.eng]
        engmap = {"pe": "tensor", "dve": "vector", "act": "scalar", "pool": "gpsimd", "sp": "sync"}
        with contextlib.ExitStack() as st:
            sems = {e: st.enter_context(nc.semaphore("s_" + e)) for e in self.ENGS}
            ssems = {s: st.enter_context(nc.semaphore("d_%d" % i)) for i, s in enumerate(self.streams)}
            block = st.enter_context(nc.Block())
            per_eng = {e: [o for o in ops if o.eng == e] for e in self.ENGS}
            out_streams = [s for s in self.streams if isinstance(s, tuple) and s[0] == "out"]

            def body(ename):
                def f(eng):
                    seen = {}
                    for o in per_eng[ename]:
                        for d in o.deps:
                            p = ops[d]
                            if is_dma(p):
                                sem, val = ssems[p.stream], p.cnt
                            else:
                                if skip(p, o):
                                    continue
                                sem, val = sems[p.eng], p.cnt
                            key = id(sem)
                            if seen.get(key, 0) >= val:
                                continue
                            seen[key] = val
                            eng.wait_ge(sem, val)
                        ins = o.fn(eng)
                        if is_dma(o):
                            ins.then_inc(ssems[o.stream], 16)
                        elif o.sig:
                            ins.then_inc(sems[o.eng], 1)
                    if ename == "sp":
                        for s in out_streams:
                            eng.wait_ge(ssems[s], scnt[s])
                return f

            for e in self.ENGS:
                if per_eng[e] or e == "sp":
                    getattr(block, engmap[e])(body(e))
        return nc


CO_MU, CO_W0, CO_A0, CO_KK, CO_KA, CO_RK, CO_GG, CO_GB, CO_SD, CO_GLB, CO_OMKA = 0, 27, 35, 43, 51, 59, 67, 75, 83, 91, 99
NCOL = 107
K_ID, K_BO, K_MASK, K_LT, K_CM, K_TAU, K_R0, K_PI2 = 0, 128, 256, 384, 448, 576, 640, 704
NCONST = 705


import os
STOP = int(os.environ.get("KSTOP", "9"))
SUB = int(os.environ.get("KSUB", "9"))
SUBB = int(os.environ.get("KSUBB", "9"))
KS5 = int(os.environ.get("KS5", "9"))


def build(NPRE, NOWN, NSAMP_TILES=2):
    nc = bass.Bass("TRN2", target_bir_lowering=False)
    kb = KB(nc)
    NTP = NPRE + NOWN
    di = lambda n, s: nc.dram_tensor(n, s, F32, kind="ExternalInput").ap()
    do = lambda n, s: nc.dram_tensor(n, s, F32, kind="ExternalOutput").ap()
    xp = di("xp", [NTP * T, D]); xs = di("xs", [NSAMP_TILES * T, D]); flag = di("flag", [128, 1])
    cshift = di("cshift", [2 * NSAMP_TILES, 27 * 128])
    swkv = di("swkv", [2 * NSAMP_TILES, 128, 8 * 64])
    s5st = di("s5st", [2 * NSAMP_TILES, 2, 128, 32])
    w_in = di("w_in", [D, NIN]); w_out = di("w_out", [D, D]); w1 = di("w1", [D, DFF]); w3 = di("w3", [D, DFF])
    w2 = di("w2", [DFF, D]); glu_w = di("glu_w", [1024, 1024])
    lora_d = di("lora", [3, 128, 1024])
    cols_d = di("cols", [128, NCOL]); consts_d = di("consts", [128, NCONST]); rows_d = di("rows", [6, D])
    s5a_d = di("s5a", [128, 3 * 32]); s5q_d = di("s5q", [128, 3 * 1024])
    bblk_d = di("bblk", [128, 2 * 1024]); cblk_d = di("cblk", [128, 2 * 1024])
    NSQ = 2 * NSAMP_TILES + 1
    scr = lambda n, shp: nc.dram_tensor(n, shp, BF16, kind="Internal").ap()
    w_out_b = scr("w_out_b", [D, D]); w2_b = scr("w2_b", [DFF, D])

    def mkblocked(name, R, blocks):
        tbl = {}
        off = 0
        for c0, ncol in blocks:
            tbl[c0] = (off, ncol)
            off += R * ncol
        return (scr(name, [off]), tbl, R // 128)

    w_in_blocks = [(i * 256, 256) for i in range(12)] + [(3072, 256), (3328, 32)] + [(3360 + i * 256, 256) for i in range(4)]
    ffn_blocks = [((qt * 11 + fl) * 128, 256 if fl < 10 else 128) for qt in range(4) for fl in range(0, 11, 2)]
    w_in_b = mkblocked("w_in_b", D, w_in_blocks); w1_b = mkblocked("w1_b", D, ffn_blocks); w3_b = mkblocked("w3_b", D, ffn_blocks)
    glu_b_ = mkblocked("glu_w_b", 1024, [(i * 256, 256) for i in range(4)])
    yp = do("yp", [NOWN * T, D]); ys_o = do("ys", [NSAMP_TILES * T, D])
    shift_o = do("shift_o", [NSQ, 27, 128]); wkv_o = do("wkv_o", [NSQ, 128, 512]); s5_o = do("s5_o", [NSQ, 2, 128, 32])

    st = contextlib.ExitStack()
    sb = lambda n, s, dt=F32: st.enter_context(nc.sbuf_tensor(n, s, dt))
    with st:
        CONST = sb("CONST", [128, NCONST]); COLS = sb("COLS", [128, NCOL]); FLAG = sb("FLAG", [128, 1]); NCOLS = sb("NCOLS", [128, NCOL])
        LORA = sb("LORA", [128, 3, 1024])
        COS = sb("COS", [128, 32, 64]); SIN = sb("SIN", [128, 32, 64]); RHO0 = sb("RHO0", [128, 32, 64])
        RHO = sb("RHO", [128, 32])
        BRE = sb("BRE", [128, 1024]); BIM = sb("BIM", [128, 1024]); CRE = sb("CRE", [128, 1024]); NCIM = sb("NCIM", [128, 1024])
        XTS = [sb("XT0", [128, D]), sb("XT1", [128, D])]; XF = sb("XF", [128, 16, T], BF16)
        WS = [sb("WS0", [128, 4096]), sb("WS1", [128, 4096])]
        ROWS = sb("ROWS", [128, 1, D])
        PR = sb("PR", [128, 3, 8, PAD + T])
        AR_ = sb("ARENA", [128, 6144]); AR2 = sb("ARENA2", [128, 5632])
        UF = sb("UF", [128, 8, T]); OF = sb("OF", [128, 16, T], BF16)
        STT_ = sb("STATE", [128, 8, 64]); HC = sb("HC", [128, 2, 32])
        PREV = sb("PREV", [128, 27]); SH0 = sb("SH0", [128, 27, 2]); SHO = sb("SHO", [128, 27, NSQ])
        SMALL = sb("SMALL", [128, 96]); STATS = sb("STATS", [128, 4, 6])
        PSA = st.enter_context(nc.psum_tensor("PSA", [128, 2048], F32))
        PSB = st.enter_context(nc.psum_tensor("PSB", [128, 2048], F32))

        def bank(b):
            return (PSA if b < 4 else PSB)[:, (b % 4) * 512:(b % 4) * 512 + 512]

        bank_rr = [0]
        STk = [("ST", h_) for h_ in range(8)]
        HCk = [("HC", q_) for q_ in range(4)]

        def nb():
            bank_rr[0] = (bank_rr[0] + 1) % 6
            return bank_rr[0]

        ident = CONST[:, K_ID:K_ID + 128]; bones = CONST[:, K_BO:K_BO + 128]; MASK = CONST[:, K_MASK:K_MASK + 128]
        LT = CONST[0:64, K_LT:K_LT + 64]; CM = CONST[:, K_CM:K_CM + 128]; TAU = CONST[:, K_TAU:K_TAU + 64]
        R0M = CONST[:, K_R0:K_R0 + 64]; PI2 = CONST[:, K_PI2:K_PI2 + 1]
        col = lambda o, i=0, n=1: COLS[:, o + i:o + i + n]

        ld = [0]

        def load(q, dst, src, key):
            ld[0] += 1
            kb.dma(q, dst, src, w=[key], stream=("ld", ld[0]))

        load("sp", CONST[:], consts_d, "CONST"); load("sp", COLS[:], cols_d, "COLS"); load("sp", FLAG[:], flag, "FLAG")
        load("sp", LORA[:], lora_d.rearrange("a p c -> p a c"), "LORA")
        kb.ts("dve", COLS[:, CO_OMKA:CO_OMKA + 8], COLS[:, CO_KA:CO_KA + 8], -1.0, ALU.mult, 1.0, ALU.add, r=["COLS"], w=["COLS"])
        kb.ts("dve", NCOLS[:], COLS[:], -1.0, ALU.mult, r=["COLS"], w=["COLS"])
        ncol_ = lambda o, i=0: NCOLS[:, o + i:o + i + 1]

        def sigm(eng2, out, in_, r, w, negbias=None, scale=1.0):
            if negbias is None:
                kb.act(out, in_, AF.Exp, r=r, w=w, scale=-scale)
            else:
                kb.act(out, in_, AF.Exp, r=r + ["COLS"], w=w, scale=-scale, bias=negbias)
            kb.ts(eng2, out, out, 1.0, ALU.add, r=w, w=w)
            kb.op("dve", lambda e: e.reciprocal(out=out, in_=out), r=w, w=w)

        def rsqrt_(out, in_, r, w):
            kb.act(out, in_, AF.Ln, r=r, w=w)
            kb.act(out, out, AF.Exp, r=w, w=w, scale=-0.5)
        PRf = PR[:].rearrange("p a b c -> p (a b c)")
        S5A = PRf[:, 0:96]; tmpa = [PRf[:, 128 + i * 32:128 + (i + 1) * 32] for i in range(6)]
        load("sp", S5A, s5a_d, ("PR", "s5a"))
        arr = [AR_[:, i * 1024:(i + 1) * 1024] for i in range(6)]
        prr = [PRf[:, 512 + i * 1024:512 + (i + 1) * 1024] for i in range(4)]
        wsr = [WS[0][:, i * 1024:(i + 1) * 1024] for i in range(4)] + [WS[1][:, i * 1024:(i + 1) * 1024] for i in range(4)]
        load("sp", WS[0][:, 0:3072], s5q_d, ("WS", 0))
        load("sp", WS[1][:, 0:2048], bblk_d, ("WS", 1))
        load("sp", CRE[:], cblk_d[:, 0:1024], "CRE"); load("sp", NCIM[:], cblk_d[:, 1024:2048], "NCIM")
        kb.ts("pool", NCIM[:], NCIM[:], -1.0, ALU.mult, r=["NCIM"], w=["NCIM"])
        kb.op("pool", lambda e: e.memset(PR[:], 0.0), w=[("PR", "all")])

        def sincos(src, n, frac, tmpf, sin_out, cos_out, kin, kout, ktmp):
            kb.ts("dve", frac, src, 1.0 / (2 * math.pi), ALU.mult, r=kin, w=ktmp)
            kb.cp("dve", tmpf.bitcast(I32), frac, r=ktmp, w=ktmp)
            kb.cp("dve", tmpf, tmpf.bitcast(I32), r=ktmp, w=ktmp)
            kb.tt("dve", frac, frac, tmpf, ALU.subtract, r=ktmp, w=ktmp)
            kb.ts("dve", tmpf, frac, 0.5, ALU.is_gt, r=ktmp, w=ktmp)
            kb.tt("dve", frac, frac, tmpf, ALU.subtract, r=ktmp, w=ktmp)
            kb.ts("dve", tmpf, frac, -0.5, ALU.is_lt, r=ktmp, w=ktmp)
            kb.tt("dve", frac, frac, tmpf, ALU.add, r=ktmp, w=ktmp)
            kb.act(sin_out, frac, AF.Sin, r=ktmp, w=kout, scale=2 * math.pi)
            kb.ts("dve", tmpf, frac, -1.0, ALU.mult, r=ktmp, w=ktmp)
            kb.tt("dve", tmpf, tmpf, frac, ALU.max, r=ktmp, w=ktmp)
            kb.act(cos_out, tmpf, AF.Sin, r=ktmp + ["CONST"], w=kout, scale=-2 * math.pi, bias=PI2)

        kS = [("PR", "s5a")]
        a_re, a_im, ldt = S5A[:, 0:32], S5A[:, 32:64], S5A[:, 64:96]
        dt_, ard, th = tmpa[0], tmpa[1], tmpa[2]
        kb.act(dt_, ldt, AF.Exp, r=kS, w=[("PR", "t0")])
        kb.tt("dve", ard, a_re, dt_, ALU.mult, r=kS + [("PR", "t0")], w=[("PR", "t1")])
        kb.act(RHO[:], ard, AF.Exp, r=[("PR", "t1")], w=["RHO"])
        kb.tt("dve", th, a_im, dt_, ALU.mult, r=kS + [("PR", "t0")], w=[("PR", "t2")])
        ANG = AR_[:, 0:2048]
        kb.tt("dve", ANG.rearrange("p (a b) -> p a b", a=32), th.unsqueeze(2).broadcast_to([128, 32, 64]),
              TAU.unsqueeze(1).broadcast_to([128, 32, 64]), ALU.mult, r=[("PR", "t2"), "CONST"], w=[("ARENA", "ang")])
        sincos(ANG, 2048, AR_[:, 2048:4096], AR_[:, 4096:6144], SIN[:].rearrange("p a b -> p (a b)"),
               COS[:].rearrange("p a b -> p (a b)"), [("ARENA", "ang")], ["SINCOS"], [("ARENA", "sc")])
        kb.tt("dve", RHO0[:], RHO[:].unsqueeze(2).broadcast_to([128, 32, 64]), R0M.unsqueeze(1).broadcast_to([128, 32, 64]),
              ALU.mult, r=["RHO", "CONST"], w=["RHO0"])
        kq = [("WS", 0)]
        areq, aimq, ldq = wsr[0], wsr[1], wsr[2]
        dtq, rhoq, thq, sinq, cosq, t1_, t2_ = prr[0], prr[1], prr[2], prr[3], wsr[3], wsr[6], wsr[7]
        kA = [("ARENA", "q")]
        kb.fence({"ARENA"}, "dve", lambda e: e.memset(SMALL[:, 0:1], 0.0))
        kb.act(dtq, ldq, AF.Exp, r=kq, w=[("PR", "dtq")])
        kb.tt("dve", rhoq, areq, dtq, ALU.mult, r=kq + [("PR", "dtq")], w=[("PR", "rhoq")])
        kb.act(rhoq, rhoq, AF.Exp, r=[("PR", "rhoq")], w=[("PR", "rhoq")])
        kb.tt("dve", thq, aimq, dtq, ALU.mult, r=kq + [("PR", "dtq")], w=[("PR", "thq")])
        sincos(thq, 1024, arr[0], arr[1], sinq, cosq, [("PR", "thq")], [("PR", "sinq"), ("WS", "cosq")], kA)
        nr, ni, den = arr[2], arr[3], arr[4]
        kb.tt("dve", nr, rhoq, cosq, ALU.mult, r=[("PR", "rhoq"), ("WS", "cosq")], w=[("ARENA", "nr")])
        kb.ts("dve", nr, nr, -1.0, ALU.add, r=[("ARENA", "nr")], w=[("ARENA", "nr")])
        kb.tt("dve", ni, rhoq, sinq, ALU.mult, r=[("PR", "rhoq"), ("PR", "sinq")], w=[("ARENA", "ni")])
        kb.tt("dve", den, areq, areq, ALU.mult, r=kq, w=[("ARENA", "den")])
        kb.tt("dve", arr[5], aimq, aimq, ALU.mult, r=kq, w=[("ARENA", "d2")])
        kb.tt("dve", den, den, arr[5], ALU.add, r=[("ARENA", "den"), ("ARENA", "d2")], w=[("ARENA", "den")])
        kb.op("dve", lambda e: e.reciprocal(out=den, in_=den), r=[("ARENA", "den")], w=[("ARENA", "den")])
        cre, cim = arr[0], arr[1]
        kb.tt("dve", cre, nr, areq, ALU.mult, r=[("ARENA", "nr")] + kq + kA, w=[("ARENA", "cre")])
        kb.tt("dve", arr[5], ni, aimq, ALU.mult, r=[("ARENA", "ni")] + kq, w=[("ARENA", "d2")])
        kb.tt("dve", cre, cre, arr[5], ALU.add, r=[("ARENA", "cre"), ("ARENA", "d2")], w=[("ARENA", "cre")])
        kb.tt("dve", cre, cre, den, ALU.mult, r=[("ARENA", "cre"), ("ARENA", "den")], w=[("ARENA", "cre")])
        kb.tt("dve", cim, ni, areq, ALU.mult, r=[("ARENA", "ni")] + kq + kA, w=[("ARENA", "cim")])
        kb.tt("dve", arr[5], nr, aimq, ALU.mult, r=[("ARENA", "nr")] + kq, w=[("ARENA", "d2")])
        kb.tt("dve", cim, cim, arr[5], ALU.subtract, r=[("ARENA", "cim"), ("ARENA", "d2")], w=[("ARENA", "cim")])
        kb.tt("dve", cim, cim, den, ALU.mult, r=[("ARENA", "cim"), ("ARENA", "den")], w=[("ARENA", "cim")])
        bre_b, bim_b = wsr[4], wsr[5]
        kb.tt("dve", BRE[:], cre, bre_b, ALU.mult, r=[("ARENA", "cre"), ("WS", 1)], w=["BRE"])
        kb.tt("dve", arr[5], cim, bim_b, ALU.mult, r=[("ARENA", "cim"), ("WS", 1)], w=[("ARENA", "d2")])
        kb.tt("dve", BRE[:], BRE[:], arr[5], ALU.subtract, r=["BRE", ("ARENA", "d2")], w=["BRE"])
        kb.tt("dve", BIM[:], cre, bim_b, ALU.mult, r=[("ARENA", "cre"), ("WS", 1)], w=["BIM"])
        kb.tt("dve", arr[5], cim, bre_b, ALU.mult, r=[("ARENA", "cim"), ("WS", 1)], w=[("ARENA", "d2")])
        kb.tt("dve", BIM[:], BIM[:], arr[5], ALU.add, r=["BIM", ("ARENA", "d2")], w=["BIM"])
        kb.op("pool", lambda e: e.memset(STT_[:], 0.0), w=STk)
        kb.op("pool", lambda e: e.memset(HC[:], 0.0), w=HCk)
        kb.op("pool", lambda e: e.memset(PREV[:], 0.0), w=["PREV"])
        kb.op("pool", lambda e: e.memset(SHO[:], 0.0), w=["SHO"])
        kb.fence({"PR", "ARENA", "WS"}, "dve", lambda e: e.memset(SMALL[:, 0:1], 0.0))

        pc = [0]

        def store_piece(q_, outb, dest, rc, lo, hi, r_keys, w_key, stream):
            if not isinstance(dest, tuple):
                kb.dma(q_, dest[rc * 128:(rc + 1) * 128, lo:hi], outb[:, 0:hi - lo], r=r_keys, w=[w_key], stream=stream)
                return
            flat, tbl, ndk = dest
            for c0, (off, ncol) in tbl.items():
                a, b_ = max(lo, c0), min(hi, c0 + ncol)
                if a >= b_:
                    continue
                view = flat[off:off + 128 * ndk * ncol].rearrange("(p a b) -> p a b", p=128, a=ndk)
                kb.dma(q_, view[:, rc, a - c0:b_ - c0], outb[:, a - lo:b_ - lo], r=r_keys, w=[w_key + (c0,)], stream=stream)

        def precast(wd, wsb, R, C, name):
            npiece = -(-C // 3072)
            cw = C // npiece
            assert cw * npiece == C
            for rc in range(R // 128):
                for pi_ in range(npiece):
                    s_ = pc[0] % 2
                    i_ = pc[0]
                    pc[0] += 1
                    inb = AR_[:, s_ * 3072:s_ * 3072 + cw]
                    outb = WS[s_][:].bitcast(BF16)[:, 0:cw]
                    kin = ("ARENA", "pin", s_)
                    kout = [("WS", 2 * s_), ("WS", 2 * s_ + 1)]
                    kb.dma("sp" if i_ % 2 else "act", inb, wd[rc * 128:(rc + 1) * 128, pi_ * cw:(pi_ + 1) * cw], w=[kin], stream=("pi", s_))
                    kb.cp(("dve", "act", "pool")[i_ % 3], outb, inb, r=[kin], w=kout)
                    store_piece("act" if i_ % 2 else "sp", outb, wsb, rc, pi_ * cw, (pi_ + 1) * cw, kout, ("SCR", name, i_), ("cs", s_))

        precast(w_in, w_in_b, D, NIN, "w_in")
        kb.fence({"ARENA", "WS", "SCR"}, "dve", lambda e: e.memset(SMALL[:, 0:1], 0.0))

        pc_jobs = []
        for wd_, wb_, R_, C_, nm_ in ((w_out, w_out_b, D, D, "w_out"), (w1, w1_b, D, DFF, "w1"), (w3, w3_b, D, DFF, "w3"),
                                      (w2, w2_b, DFF, D, "w2"), (glu_w, glu_b_, 1024, 1024, "glu")):
            npiece_ = -(-C_ // 1536)
            cw_ = C_ // npiece_
            assert cw_ * npiece_ == C_
            for rc_ in range(R_ // 128):
                for pi_ in range(npiece_):
                    pc_jobs.append((wd_[rc_ * 128:(rc_ + 1) * 128, pi_ * cw_:(pi_ + 1) * cw_], (wb_, rc_, pi_ * cw_, (pi_ + 1) * cw_), cw_, nm_))
        pc_pos = [0]
        OFk_all = [("OF", i_) for i_ in range(16)]

        def pc_emit(n):
            PRfl = PR[:].rearrange("p a b c -> p (a b c)")
            OFfl = OF[:].rearrange("p a b -> p (a b)")
            for _ in range(n):
                if pc_pos[0] >= len(pc_jobs):
                    return
                src_, dst_, cw_, nm_ = pc_jobs[pc_pos[0]]
                i_ = pc_pos[0]
                pc_pos[0] += 1
                kb.dma("sp", PRfl[:, 0:cw_], src_, w=[("PR", "r")], stream=("pj", 0))
                kb.cp("pool", OFfl[:, 0:cw_], PRfl[:, 0:cw_], r=[("PR", "r")], w=OFk_all)
                store_piece("sp", OFfl[:, 0:cw_], dst_[0], dst_[1], dst_[2], dst_[3], OFk_all, ("SCR", nm_, "d", i_), ("pk", 0))

        wsn = [0]
        ost = [0]

        WSB = [WS[s_ // 2][:, (s_ % 2) * 2048:(s_ % 2 + 1) * 2048].bitcast(BF16) for s_ in range(4)]

        def wload(src_ap, shape_view):
            s = wsn[0] % 4
            wsn[0] += 1
            key = ("WS", s)
            kb.dma("sp", shape_view(WSB[s]), src_ap, r=[("SCR", "x")], w=[key], stream=("ws", s))
            return shape_view(WSB[s]), key

        def colblock(wblk, nrow_chunks, c0, ncols):
            flat, tbl, ndk = wblk
            off, ncol_ = tbl[c0]
            assert ncol_ == ncols and ndk == nrow_chunks, (c0, ncols, ncol_)
            src = flat[off:off + 128 * ndk * ncols].rearrange("(p a b) -> p a b", p=128, a=ndk)
            return wload(src, lambda t: t[:, 0:nrow_chunks * ncols].rearrange("p (a b) -> p a b", a=nrow_chunks))

        def rowblock(wdram, r0, nchunks):
            src = wdram[r0:r0 + nchunks * 128, :].rearrange("(a p) c -> p a c", p=128)
            return wload(src, lambda t: t[:, 0:nchunks * D].rearrange("p (a b) -> p a b", a=nchunks))

        def layer_norm_tm(XT, rows_g, rows_b, keyx):
            for c in range(4):
                kb.op("dve", lambda e, c=c, src_=XT[:, c * 512:(c + 1) * 512]: e.bn_stats(out=STATS[:, c, :], in_=src_), r=[keyx], w=["STATS"])
            mv = SMALL[:, 2:4]; rs = SMALL[:, 4:5]
            kb.op("dve", lambda e: e.bn_aggr(out=mv, in_=STATS[:].rearrange("p a b -> p (a b)")), r=["STATS"], w=["mv"])
            kb.ts("dve", rs, mv[:, 1:2], LN_EPS, ALU.add, r=["mv"], w=["rs"])
            rsqrt_(rs, rs, ["rs"], ["rs"])
            kb.ts("dve", XT[:], XT[:], mv[:, 0:1], ALU.subtract, rs, ALU.mult, r=[keyx, "mv", "rs"], w=[keyx])
            kb.dma("sp", ROWS[:, 0, :], rows_d[rows_g:rows_g + 1, :].broadcast_to([128, D]), w=[("ROWS", 0)], stream=("rw", 0))
            kb.tt("dve", XT[:], XT[:], ROWS[:, 0, :], ALU.mult, r=[keyx, ("ROWS", 0)], w=[keyx])
            kb.dma("sp", ROWS[:, 0, :], rows_d[rows_b:rows_b + 1, :].broadcast_to([128, D]), w=[("ROWS", 0)], stream=("rw", 0))
            kb.tt("pool", XT[:], XT[:], ROWS[:, 0, :], ALU.add, r=[keyx, ("ROWS", 0)], w=[keyx])

        def to_fm(XT, keyx):
            for g4 in range(4):
                b = nb()
                for j in range(4):
                    dk = g4 * 4 + j
                    kb.tr(bank(b)[:, j * 128:(j + 1) * 128], XT[:, dk * 128:(dk + 1) * 128], ident, r=[keyx, "CONST"], w=[("ps", b)])
                kb.cp("act" if g4 % 2 else "dve", XF[:, g4 * 4:(g4 + 1) * 4, :],
                      bank(b).rearrange("p (a b) -> p a b", a=4), r=[("ps", b)], w=[("XF", g4)])

        XFk = [("XF", g) for g in range(4)]
        STk = [("ST", h_) for h_ in range(8)]

        prefetched = set()

        def prefetch_x(d):
            xsrc_, tix_ = d[0], d[1]
            XTn = XTS[tix_ % 2]
            kxn = "XT%d" % (tix_ % 2)
            kb.dma("sp", XTn[:], xsrc_, w=[kxn], stream=("x", tix_ % 2))
            layer_norm_tm(XTn, 0, 1, kxn)
            prefetched.add(tix_)

        def tile(xsrc, tix, full, seqs, yout, first_own, last_pre=False, nxt=None):
            XT = XTS[tix % 2]
            kx = "XT%d" % (tix % 2)
            sample = seqs is not None
            if STOP < 1:
                return
            kb.fence({"PR", "ARENA", "ARENA2", "WS", "WSS"}, "dve", lambda e: e.memset(SMALL[:, 0:1], 0.0))
            if first_own or (sample and pc_pos[0] < len(pc_jobs)):
                pc_emit(len(pc_jobs))
                kb.fence({"SCR", "PR"}, "dve", lambda e: e.memset(SMALL[:, 0:1], 0.0))
            if first_own:
                kb.ts("dve", STT_[:], STT_[:], FLAG[:], ALU.mult, r=STk + ["FLAG"], w=STk)
                kb.ts("dve", HC[:], HC[:], FLAG[:], ALU.mult, r=HCk + ["FLAG"], w=HCk)
                kb.ts("dve", PREV[:], PREV[:], FLAG[:], ALU.mult, r=["PREV", "FLAG"], w=["PREV"])
            if tix not in prefetched:
                prefetch_x((xsrc, tix))
            if sample:
                csv = WS[0][0:2, 0:3456]
                kb.dma("sp", csv, cshift[seqs[0]:seqs[0] + 2, :], w=[("WS", 0), ("WS", 1)], stream=("ws", 0))
                csk = ("WS", 0)
                b = nb()
                for fc in range(27):
                    kb.tr(bank(b)[:, fc * 2:fc * 2 + 2], csv[:, fc * 128:(fc + 1) * 128], ident[0:2, 0:2],
                          r=[("WS", 0), ("WS", 1), "CONST"], w=[("ps", b)])
                kb.cp("dve", SH0[:], bank(b)[:, 0:54].rearrange("p (a b) -> p a b", a=27), r=[("ps", b)], w=["SH0"])
                kb.cp("dve", PREV[:], SH0[:, :, 0], r=["SH0"], w=["PREV"])
            to_fm(XT, kx)

            if STOP < 2:
                return
            RAWG = AR_[:, 0:8 * (T + 1)].rearrange("p (a b) -> p a b", a=8)
            DG_ = AR_[:, 1100:1100 + 8 * T].rearrange("p (a b) -> p a b", a=8)
            groups = [("k", 1024, 8, 8, 1), ("v", 2048, 8, 16, 2), ("l", 3072, 3, 24, None), ("u", 3360, 8, None, None)]
            fullp1 = full or last_pre
            if fullp1:
                groups = [("r", 0, 8, 0, 0)] + groups
            for gname, c0, nfc, mu0, pri in groups:
                kraw = ("ARENA", "raw")
                if gname != "u":
                    kb.cp("pool", RAWG[:, 0:nfc, 0], PREV[:, mu0:mu0 + nfc], r=["PREV"], w=[kraw])
                for blk in range((nfc + 1) // 2):
                    ncol = 256 if not (gname == "l" and blk == 1) else 32
                    if gname == "l" and blk == 1 and not fullp1:
                        continue
                    wv, wk = colblock(w_in_b, 16, c0 + blk * 256, ncol)
                    for j in range(2 if ncol == 256 else 1):
                        fc = blk * 2 + j
                        mrows = 128 if ncol == 256 else 32
                        if gname == "l" and not fullp1 and fc > 0:
                            continue
                        b = nb()
                        for dk in range(16):
                            kb.mm(bank(b)[0:mrows, 0:T], wv[:, dk, j * 128:j * 128 + mrows], XF[:, dk, :], start=(dk == 0), stop=(dk == 15),
                                  r=[wk] + XFk, w=[("ps", b)])
                        if gname == "u":
                            kb.cp("act", UF[:, fc, :], bank(b)[:, 0:T], r=[("ps", b)], w=[("UF", fc)])
                        else:
                            kb.cp("act", RAWG[0:mrows, fc, 1:T + 1], bank(b)[0:mrows, 0:T], r=[("ps", b)], w=[kraw])
                if gname == "u":
                    continue
                n_ = nfc if (fullp1 or gname != "l") else 1
                mu_b = COLS[:, mu0:mu0 + n_].unsqueeze(2).broadcast_to([128, n_, T])
                dst = PR[:, pri, :, PAD:PAD + T] if pri is not None else AR_[:, 2200:2200 + 3 * T].rearrange("p (a b) -> p a b", a=3)[:, 0:n_, :]
                kdst = ("PR", gname)
                kb.tt("dve", DG_[:, 0:n_, :], RAWG[:, 0:n_, 0:T], RAWG[:, 0:n_, 1:T + 1], ALU.subtract, r=[kraw], w=[("ARENA", "dg")])
                kb.tt("dve", DG_[:, 0:n_, :], DG_[:, 0:n_, :], mu_b, ALU.mult, r=[("ARENA", "dg"), "COLS"], w=[("ARENA", "dg")])
                kb.tt("dve", dst, DG_[:, 0:n_, :], RAWG[:, 0:n_, 1:T + 1], ALU.add, r=[("ARENA", "dg"), kraw], w=[kdst])
                if sample:
                    t0 = L
                    tmp = SMALL[:, 8:8 + n_]
                    kb.tt("dve", tmp, SH0[:, mu0:mu0 + n_, 1], RAWG[:, 0:n_, 1 + t0], ALU.subtract, r=["SH0", kraw], w=["sm8"])
                    kb.tt("dve", tmp, tmp, COLS[:, mu0:mu0 + n_], ALU.mult, r=["sm8", "COLS"], w=["sm8"])
                    kb.tt("dve", dst[:, :, t0], tmp, RAWG[:, 0:n_, 1 + t0], ALU.add, r=["sm8", kraw], w=[kdst])
                    for c in range(2):
                        kb.cp("pool", SHO[:, mu0:mu0 + n_, seqs[c]], RAWG[:, 0:n_, (c + 1) * L], r=[kraw], w=["SHO"])
                else:
                    kb.cp("pool", PREV[:, mu0:mu0 + n_], RAWG[:, 0:n_, T], r=[kraw], w=["PREV"])
                    if full:
                        kb.cp("pool", SHO[:, mu0:mu0 + n_, NSQ - 1], RAWG[:, 0:n_, T], r=[kraw], w=["SHO"])
            LT_ = AR_[:, 2200:2200 + 3 * T].rearrange("p (a b) -> p a b", a=3)
            kL = ("PR", "l")
            kb.act(LT_[0:64, 0, :], LT_[0:64, 0, :], AF.Tanh, r=[kL], w=[kL])
            if full:
                kb.act(LT_[:, 1, :], LT_[:, 1, :], AF.Sigmoid, r=[kL], w=[kL])
                kb.act(LT_[0:32, 2, :], LT_[0:32, 2, :], AF.Sigmoid, r=[kL], w=[kL])

            if STOP < 3:
                return
            YS = XF[:].rearrange("p a b -> p (a b)").bitcast(F32).rearrange("p (a b) -> p a b", a=8)
            def s5_gen(c, q, Wt_, Kq, smo):
                ps_re = bank(6); ps_im = bank(7)
                kre = [("ps", 6)]; kim = [("ps", 7)]
                W = [Wt_[:, i * 512:(i + 1) * 512] for i in range(6)]
                p3 = lambda x: x.rearrange("p (a b) -> p a b", a=8)
                sl8 = slice(q * 8, (q + 1) * 8)
                cosv = COS[:, sl8, :].rearrange("p a b -> p (a b)")
                sinv = SIN[:, sl8, :].rearrange("p a b -> p (a b)")
                rho0 = RHO0[:, sl8, :].rearrange("p a b -> p (a b)")
                kb.tt("dve", W[0], ps_re, cosv, ALU.mult, r=kre + ["SINCOS"], w=[Kq("w0")])
                kb.tt("dve", W[1], ps_im, sinv, ALU.mult, r=kim + ["SINCOS"], w=[Kq("w1")])
                kb.tt("dve", W[2], ps_im, cosv, ALU.mult, r=kim + ["SINCOS"], w=[Kq("w2")])
                kb.tt("dve", W[3], ps_re, sinv, ALU.mult, r=kre + ["SINCOS"], w=[Kq("w3")])
                yield
                kb.tt("pool", W[0], W[0], W[1], ALU.add, r=[Kq("w0"), Kq("w1")], w=[Kq("w0")])
                kb.tt("pool", W[2], W[2], W[3], ALU.subtract, r=[Kq("w2"), Kq("w3")], w=[Kq("w2")])
                for ri, Wt, kk_ in ((0, W[0], "w0"), (1, W[2], "w2")):
                    tmp = SMALL[:, smo + 8 * ri:smo + 8 + 8 * ri]
                    kb.tt("pool", tmp, HC[:, ri, sl8], RHO[:, sl8], ALU.mult, r=[("HC", q), "RHO"], w=[("smh", smo, ri)])
                    kb.tt("pool", p3(Wt)[:, :, 0], p3(Wt)[:, :, 0], tmp, ALU.add, r=[("smh", smo, ri), Kq(kk_)], w=[Kq(kk_)])
                yield
                kb.op("dve", lambda e, o=W[1], d=W[0], r0=rho0: e.tensor_tensor_scan(out=o, data0=r0, data1=d, initial=0.0, op0=ALU.mult, op1=ALU.add),
                      r=[Kq("w0"), "RHO0"], w=[Kq("w1")])
                kb.op("dve", lambda e, o=W[3], d=W[2], r0=rho0: e.tensor_tensor_scan(out=o, data0=r0, data1=d, initial=0.0, op0=ALU.mult, op1=ALU.add),
                      r=[Kq("w2"), "RHO0"], w=[Kq("w3")])
                yield
                gre, gim = W[1], W[3]
                if not full:
                    cl = COS[:, sl8, L - 1]; sl_ = SIN[:, sl8, L - 1]
                    gr = p3(gre)[:, :, L - 1]; gi_l = p3(gim)[:, :, L - 1]
                    t_ = [SMALL[:, smo + 16 + 8 * i:smo + 24 + 8 * i] for i in range(2)]
                    kb.tt("pool", t_[0], gr, cl, ALU.mult, r=[Kq("w1"), "SINCOS"], w=[("smt", smo, 0)])
                    kb.tt("pool", t_[1], gi_l, sl_, ALU.mult, r=[Kq("w3"), "SINCOS"], w=[("smt", smo, 1)])
                    kb.tt("pool", HC[:, 0, sl8], t_[0], t_[1], ALU.subtract, r=[("smt", smo, 0), ("smt", smo, 1)], w=[("HC", q)])
                    kb.tt("pool", t_[0], gi_l, cl, ALU.mult, r=[Kq("w3"), "SINCOS", ("HC", q)], w=[("smt", smo, 0)])
                    kb.tt("pool", t_[1], gr, sl_, ALU.mult, r=[Kq("w1"), "SINCOS", ("HC", q)], w=[("smt", smo, 1)])
                    kb.tt("pool", HC[:, 1, sl8], t_[0], t_[1], ALU.add, r=[("smt", smo, 0), ("smt", smo, 1)], w=[("HC", q)])
                    return
                kb.tt("dve", W[0], gre, cosv, ALU.mult, r=[Kq("w1"), "SINCOS"], w=[Kq("w0")])
                kb.tt("pool", W[2], gim, sinv, ALU.mult, r=[Kq("w3"), "SINCOS"], w=[Kq("w2")])
                kb.tt("dve", W[4], W[0], W[2], ALU.subtract, r=[Kq("w0"), Kq("w2")], w=[Kq("w4")])
                yield
                kb.tt("pool", W[0], gim, cosv, ALU.mult, r=[Kq("w3"), "SINCOS"], w=[Kq("w0")])
                kb.tt("dve", W[2], gre, sinv, ALU.mult, r=[Kq("w1"), "SINCOS"], w=[Kq("w2")])
                kb.tt("pool", W[5], W[0], W[2], ALU.add, r=[Kq("w0"), Kq("w2")], w=[Kq("w5")])
                kb.cp("pool", HC[:, 0, sl8], p3(W[4])[:, :, L - 1], r=[Kq("w4")], w=[("HC", q)])
                kb.cp("pool", HC[:, 1, sl8], p3(W[5])[:, :, L - 1], r=[Kq("w5")], w=[("HC", q)])
                yield
                b = 6
                for fc in range(8):
                    pi_ = q * 8 + fc
                    kb.mm(bank(b)[0:32, fc * 64:(fc + 1) * 64], CRE[:, pi_ * 32:(pi_ + 1) * 32], p3(W[4])[:, fc, :], start=True, stop=False,
                          r=["CRE", Kq("w4")], w=[("ps", b)])
                    kb.mm(bank(b)[0:32, fc * 64:(fc + 1) * 64], NCIM[:, pi_ * 32:(pi_ + 1) * 32], p3(W[5])[:, fc, :], start=False, stop=True,
                          r=["NCIM", Kq("w5")], w=[("ps", b)])
                kb.cp("dve", YS[32 * q:32 * q + 32, :, c * L:(c + 1) * L], bank(b)[0:32, :].rearrange("p (a b) -> p a b", a=8),
                      r=[("ps", b)], w=XFk)


            def s5_tile_gen():
                kb.fence({"WS", "WSS"}, "dve", lambda e: e.memset(SMALL[:, 0:1], 0.0))
                for c in range(NCH):
                    if sample:
                        kb.dma("sp", HC[:], s5st[seqs[c]].rearrange("a p b -> p a b"), w=HCk, stream=("hc", 0))
                    for q in range(4):
                        kb.mm(bank(6)[:, 0:128], ident, ident, r=["CONST"], w=[("ps", 6)])
                        for fc in range(8):
                            for ri, Bt in ((0, BRE), (1, BIM)):
                                kb.mm(bank(6 + ri)[:, fc * 64:(fc + 1) * 64], Bt[32 * q:32 * q + 32, fc * 128:(fc + 1) * 128],
                                      UF[32 * q:32 * q + 32, fc, c * L:(c + 1) * L], r=["BRE", "BIM", ("UF", fc)], w=[("ps", 6 + ri)], tile_position=(32 * q, 0))
                        yield
                        yield from s5_gen(c, q, WS[0], (lambda n: ("WSS", n)), 16)
                    if sample:
                        kb.dma("act", s5_o[seqs[c]].rearrange("a p b -> p a b"), HC[:], r=HCk, stream=("out", "s5"))

            kb.fence({"ARENA"}, "dve", lambda e: e.memset(SMALL[:, 0:1], 0.0))
            NIT = 2 * NCH

            def make_scratch(arena, pref, a_base, regions):
                regs = [list(r_) for r_ in regions]
                def carve(n):
                    for r_ in regs:
                        if r_[0] + n <= r_[1]:
                            v = arena[:, r_[0]:r_[0] + n]
                            r_[0] += n
                            return v
                    raise AssertionError("scratch overflow")
                S = {}
                S["A"] = lambda i: arena[:, a_base + i * T:a_base + (i + 1) * T]
                v3_ = lambda ap, n: ap.rearrange("p (a b) -> p a b", a=n)
                for nm in ("BK", "ARt", "BKs", "DGm"):
                    S[nm] = v3_(carve(NCH * 128), NCH)
                S["Mm"] = v3_(carve(NIT * 128), NIT)
                S["NTp"] = [v3_(carve(NIT * 64), NIT) for _ in range(2)]
                S["PQ"] = [v3_(carve(NIT * 128), NIT) for _ in range(2)]
                for nm in ("XBs", "BKT", "UV"):
                    S[nm] = v3_(carve(NIT * 64), NIT)
                S["Xs"] = carve(128)
                S["K"] = lambda n: (pref, n)
                return S

            SCR = [make_scratch(AR_, "ARENA", 2600, [(0, 2200), (2600 + 13 * T, 6144)]),
                   make_scratch(AR2, "ARENA2", 0, [(13 * T, 5632)])]

            def hp_gen(hp, S):
                A = S["A"]; K = S["K"]
                BK, ARt, BKs, DGm, Mm, NTp, PQ, XBs, BKT, UV, Xs = (S[n_] for n_ in ("BK", "ARt", "BKs", "DGm", "Mm", "NTp", "PQ", "XBs", "BKT", "UV", "Xs"))
                Tm = PQ[0][:, :, 64:128]
                hc = slice(hp * 128, (hp + 1) * 128)
                r_ = PR[:, 0, hp, PAD:PAD + T]; k_ = PR[:, 1, hp, PAD:PAD + T]; v_ = PR[:, 2, hp, PAD:PAD + T]
                sg, a_, kk, kp, bb, cs, gm, gi, gp, tq = A(0), A(1), A(2), A(3), A(4), A(5), A(6), A(7), A(8), A(9)
                gg, bon, yy = A(10), A(11), A(12)
                b = nb()
                kb.mm(bank(b)[:, 0:T], LORA[0:64, 0, hc], LT_[0:64, 0, :], r=["LORA", kL], w=[("ps", b)])
                kb.act(sg, bank(b)[:, 0:T], AF.Sigmoid, r=[("ps", b), "COLS"], w=[K("sg")], bias=col(CO_W0, hp))
                b = nb()
                kb.mm(bank(b)[:, 0:T], LORA[64:128, 0, hc], LT_[64:128, 0, :], r=["LORA", kL], w=[("ps", b)])
                kb.act(a_, bank(b)[:, 0:T], AF.Sigmoid, r=[("ps", b), "COLS"], w=[K("a")], bias=col(CO_A0, hp))
                yield
                if full:
                    b = nb()
                    kb.mm(bank(b)[:, 0:T], LORA[:, 1, hc], LT_[:, 1, :], start=True, stop=False, r=["LORA", kL], w=[("ps", b)])
                    kb.mm(bank(b)[:, 0:T], LORA[:, 2, hc], LT_[:, 2, :], start=False, stop=True, r=["LORA", kL], w=[("ps", b)])
                    kb.cp("act", gg, bank(b)[:, 0:T], r=[("ps", b)], w=[K("gg")])
                kb.ts("dve", kk, k_, col(CO_KK, hp), ALU.mult, r=[("PR", "k"), "COLS"], w=[K("kk")])
                kb.tt("pool", tq, kk, kk, ALU.mult, r=[K("kk")], w=[K("tq")])
                b = nb()
                kb.mm(bank(b)[:, 0:T], bones, tq, r=["CONST", K("tq")], w=[("ps", b)])
                kb.ts("dve", tq, bank(b)[:, 0:T], 1e-24, ALU.max, r=[("ps", b)], w=[K("tq")])
                rsqrt_(tq, tq, [K("tq")], [K("tq")])
                kb.tt("dve", kk, kk, tq, ALU.mult, r=[K("kk"), K("tq")], w=[K("kk")])
                yield
                kb.ts("dve", kp, a_, col(CO_KA, hp), ALU.mult, col(CO_OMKA, hp), ALU.add, r=[K("a"), "COLS"], w=[K("kp")])
                kb.tt("dve", kp, kp, k_, ALU.mult, r=[K("kp"), ("PR", "k")], w=[K("kp")])
                kb.tt("pool", bb, kk, a_, ALU.mult, r=[K("kk"), K("a")], w=[K("bb")])
                kb.op("dve", lambda e, cs=cs, sg=sg: e.tensor_tensor_scan(out=cs, data0=CM, data1=sg, initial=0.0, op0=ALU.mult, op1=ALU.add),
                      r=["CONST", K("sg")], w=[K("cs")])
                kb.act(gm, cs, AF.Exp, r=[K("cs")], w=[K("gm")], scale=-C0)
                kb.act(gi, cs, AF.Exp, r=[K("cs")], w=[K("gi")], scale=C0)
                kb.tt("pool", gp, cs, sg, ALU.subtract, r=[K("cs"), K("sg")], w=[K("gp")])
                kb.act(gp, gp, AF.Exp, r=[K("gp")], w=[K("gp")], scale=-C0)
                yield
                v4 = lambda x: x.rearrange("p (a b) -> p a b", a=NCH)
                kb.tt("dve", BK[:, :, 0:64], v4(bb), v4(gi), ALU.mult, r=[K("bb"), K("gi")], w=[K("BK")])
                kb.tt("dve", BK[:, :, 64:128], v4(kp), v4(gi), ALU.mult, r=[K("kp"), K("gi")], w=[K("BK")])
                kb.stt(ARt[:, :, 0:64], v4(kk), -1.0, v4(gp), ALU.mult, ALU.mult, r=[K("kk"), K("gp")], w=[K("AR")])
                if full:
                    kb.tt("dve", ARt[:, :, 64:128], v4(r_), v4(gm), ALU.mult, r=[("PR", "r"), K("gm")], w=[K("AR")])
                    kb.stt(tq, r_, col(CO_RK, hp), kp, ALU.mult, ALU.mult, r=[("PR", "r"), K("kp"), "COLS", K("tq")], w=[K("tq")])
                    b = nb()
                    kb.mm(bank(b)[:, 0:T], bones, tq, r=["CONST", K("tq")], w=[("ps", b)])
                    kb.tt("dve", bon, bank(b)[:, 0:T], v_, ALU.mult, r=[("ps", b), ("PR", "v")], w=[K("bon")])
                else:
                    kb.op("pool", lambda e: e.memset(ARt[:, :, 64:128], 0.0), w=[K("AR")])
                gL = v4(gm)[:, :, L - 1:L]
                kb.tt("dve", BKs[:], BK[:], gL.broadcast_to([128, NCH, 128]), ALU.mult, r=[K("BK"), K("gm")], w=[K("BKs")])
                kb.tt("pool", DGm[:], ident.unsqueeze(1).broadcast_to([128, NCH, 128]), gL.broadcast_to([128, NCH, 128]), ALU.mult,
                      r=["CONST", K("gm")], w=[K("DG")])
                yield
                if SUB < 1:
                    return
                pr_ = lambda par: slice(par * 64, par * 64 + 64)
                IT = lambda par, c: par * NCH + c
                v3 = lambda ap, n: ap.rearrange("p (a b) -> p a b", a=n)
                bp = [nb(), nb()]
                for par in range(2):
                    for c in range(NCH):
                        kb.mm(bank(bp[par])[:, c * 128:(c + 1) * 128], BK[pr_(par), c, :], ARt[pr_(par), c, :],
                              r=[K("BK"), K("AR")], w=[("ps", bp[par])])
                for par in range(2):
                    kb.tt("dve", Mm[:, par * NCH:(par + 1) * NCH, :], v3(bank(bp[par])[:, 0:NCH * 128], NCH),
                          MASK.unsqueeze(1).broadcast_to([128, NCH, 128]), ALU.mult, r=[("ps", bp[par]), "CONST"], w=[K("Mm")])
                bp = [nb(), nb()]
                for par in range(2):
                    for c in range(NCH):
                        kb.mm(bank(bp[par])[0:64, c * 64:(c + 1) * 64], ARt[pr_(par), c, 0:64], BK[pr_(par), c, 0:64], r=[K("AR"), K("BK")], w=[("ps", bp[par])])
                for par in range(2):
                    kb.tt("dve", NTp[0][0:64, par * NCH:(par + 1) * NCH, :], v3(bank(bp[par])[0:64, 0:NCH * 64], NCH),
                          LT.unsqueeze(1).broadcast_to([64, NCH, 64]), ALU.mult, r=[("ps", bp[par]), "CONST"], w=[K(("NT", 0))])
                kb.cp("pool", PQ[0][0:64, :, 0:64], Mm[0:64, :, 0:64], r=[K("Mm")], w=[K(("PQ", 0))])
                kb.cp("pool", PQ[0][0:64, :, 64:128], ident[0:64, 0:64].unsqueeze(1).broadcast_to([64, NIT, 64]), r=["CONST", K(("PQ", 0))], w=[K(("PQ", 0))])
                yield
                if SUB < 2:
                    return
                bp = [nb(), nb()]
                for par in range(2):
                    for c in range(NCH):
                        kb.tr(bank(bp[par])[:, c * 64:(c + 1) * 64], PR[pr_(par), 2, hp, PAD + c * L - 64:PAD + c * L + 64], ident[pr_(par), pr_(par)],
                              r=[("PR", "v"), "CONST"], w=[("ps", bp[par])])
                for par in range(2):
                    kb.cp("act", UV[64:128, par * NCH:(par + 1) * NCH, :], v3(bank(bp[par])[64:128, 0:NCH * 64], NCH), r=[("ps", bp[par])], w=[K("UVv")])
                bp = [nb(), nb()]
                for par in range(2):
                    for c in range(NCH):
                        kb.tr(bank(bp[par])[:, c * 64:(c + 1) * 64], BKs[pr_(par), c, :], ident[pr_(par), pr_(par)],
                              r=[K("BKs"), "CONST"], w=[("ps", bp[par])])
                for par in range(2):
                    kb.cp("act", BKT[:, par * NCH:(par + 1) * NCH, :], v3(bank(bp[par])[:, 0:NCH * 64], NCH), r=[("ps", bp[par])], w=[K("BKT")])
                b = nb()
                for it in range(NIT):
                    kb.mm(bank(b)[0:64, it * 64:(it + 1) * 64], Mm[64:128, it, 0:64], UV[64:128, it, :], r=[K("Mm"), K("UVv")], w=[("ps", b)])
                kb.cp("act", XBs[0:64, :, :], v3(bank(b)[0:64, 0:NIT * 64], NIT), r=[("ps", b)], w=[K("XBs")])
                yield
                if SUB < 3:
                    return
                for lev in range(6):
                    cur, nxt = lev % 2, (lev + 1) % 2
                    last = lev == 5
                    b = nb()
                    for it in range(NIT):
                        kb.mm(bank(b)[0:64, it * 128:(it + 1) * 128], NTp[cur][:, it, :], PQ[cur][:, it, :], r=[K(("NT", cur)), K(("PQ", cur)), K("zpad")], w=[("ps", b)])
                    if not last:
                        b2 = nb()
                        for it in range(NIT):
                            kb.mm(bank(b2)[0:64, it * 64:(it + 1) * 64], PQ[cur][:, it, 0:64], NTp[cur][:, it, :], r=[K(("NT", cur)), K(("PQ", cur)), K("zpad")], w=[("ps", b2)])
                    pa = v3(bank(b)[0:64, 0:NIT * 128], NIT)
                    kb.tt("dve", PQ[nxt][0:64, :, 64:128], PQ[cur][0:64, :, 64:128], pa[:, :, 64:128], ALU.add, r=[("ps", b), K(("PQ", cur))], w=[K(("PQ", nxt))])
                    if not last:
                        kb.cp("act", PQ[nxt][0:64, :, 0:64], pa[:, :, 0:64], r=[("ps", b), K(("PQ", nxt))], w=[K(("PQ", nxt))])
                        kb.cp("act", NTp[nxt][0:64, :, :], v3(bank(b2)[0:64, 0:NIT * 64], NIT), r=[("ps", b2)], w=[K(("NT", nxt))])
                    yield
                if SUB < 4:
                    return
                for c in range(NCH):
                    if sample:
                        kb.dma("sp", STT_[:, hp, :], swkv[seqs[c], :, hp * 64:(hp + 1) * 64], w=[("ST", hp)], stream=("st", hp))
                    bp = [nb(), nb()]
                    for par in range(2):
                        kb.mm(bank(bp[par])[0:64, 0:64], ARt[pr_(par), c, 0:64], STT_[pr_(par), hp, :], r=[K("AR"), ("ST", hp)], w=[("ps", bp[par])])
                    for par in range(2):
                        kb.tt("dve", Xs[0:64, par * 64:(par + 1) * 64], bank(bp[par])[0:64, 0:64], XBs[0:64, IT(par, c), :], ALU.add,
                              r=[("ps", bp[par]), K("XBs")], w=[K(("Xs", par))])
                    yield
                    b = nb()
                    for par in range(2):
                        kb.mm(bank(b)[0:64, par * 64:(par + 1) * 64], Tm[0:64, IT(par, c), :], Xs[0:64, par * 64:(par + 1) * 64], r=[K(("PQ", 0)), K(("Xs", par))], w=[("ps", b)])
                    for par in range(2):
                        kb.cp("act", UV[0:64, IT(par, c), :], bank(b)[0:64, par * 64:(par + 1) * 64], r=[("ps", b)], w=[K("UVu")])
                    yield
                    if full:
                        bp = [nb(), nb()]
                        for par in range(2):
                            kb.mm(bank(bp[par])[0:64, 0:64], STT_[pr_(par), hp, :], ARt[pr_(par), c, 64:128], start=True, stop=False,
                                  r=[("ST", hp), K("AR")], w=[("ps", bp[par])])
                        for par in range(2):
                            kb.mm(bank(bp[par])[0:64, 0:64], UV[:, IT(par, c), :], Mm[:, IT(par, c), 64:128], start=False, stop=True,
                                  r=[K("UVu"), K("UVv"), K("Mm")], w=[("ps", bp[par])])
                        kb.cp("act", yy[0:64, c * L:(c + 1) * L], bank(bp[0])[0:64, 0:64], r=[("ps", bp[0])], w=[K("yy")])
                        kb.cp("dve", yy[64:128, c * L:(c + 1) * L], bank(bp[1])[0:64, 0:64], r=[("ps", bp[1])], w=[K("yy")])
                    bp = [nb(), nb()]
                    for par in range(2):
                        kb.mm(bank(bp[par])[0:64, 0:64], DGm[pr_(par), c, pr_(par)], STT_[pr_(par), hp, :], start=True, stop=False,
                              r=[K("DG"), ("ST", hp)], w=[("ps", bp[par])])
                    for par in range(2):
                        kb.mm(bank(bp[par])[0:64, 0:64], BKT[:, IT(par, c), :], UV[:, IT(par, c), :], start=False, stop=True,
                              r=[K("BKT"), K("UVu"), K("UVv")], w=[("ps", bp[par])])
                    kb.cp("act", STT_[0:64, hp, :], bank(bp[0])[0:64, 0:64], r=[("ps", bp[0])], w=[("ST", hp)])
                    kb.cp("dve", STT_[64:128, hp, :], bank(bp[1])[0:64, 0:64], r=[("ps", bp[1])], w=[("ST", hp)])
                    yield
                    if sample:
                        kb.dma("act", wkv_o[seqs[c], :, hp * 64:(hp + 1) * 64], STT_[:, hp, :], r=[("ST", hp)], stream=("out", "wkv", hp))
                if full and SUB >= 5:
                    b = nb()
                    kb.mm(bank(b)[:, 0:T], bones, yy, r=["CONST", K("yy")], w=[("ps", b)])
                    kb.stt(yy, bank(b)[:, 0:T], -1.0 / 64, yy, ALU.mult, ALU.add, r=[("ps", b), K("yy")], w=[K("yy")])
                    kb.tt("pool", tq, yy, yy, ALU.mult, r=[K("yy")], w=[K("tq")])
                    yield
                    b = nb()
                    kb.mm(bank(b)[:, 0:T], bones, tq, r=["CONST", K("tq")], w=[("ps", b)])
                    kb.ts("dve", tq, bank(b)[:, 0:T], 1.0 / 64, ALU.mult, GN_EPS, ALU.add, r=[("ps", b)], w=[K("tq")])
                    rsqrt_(tq, tq, [K("tq")], [K("tq")])
                    kb.tt("dve", yy, yy, tq, ALU.mult, r=[K("yy"), K("tq")], w=[K("yy")])
                    kb.ts("dve", yy, yy, col(CO_GG, hp), ALU.mult, col(CO_GB, hp), ALU.add, r=[K("yy"), "COLS"], w=[K("yy")])
                    kb.tt("pool", yy, yy, bon, ALU.add, r=[K("yy"), K("bon")], w=[K("yy")])
                    kb.tt("pool", OF[:, hp, :], yy, gg, ALU.mult, r=[K("yy"), K("gg")], w=[("OF", hp)])
            for S_ in SCR:
                for t_ in S_["NTp"] + S_["PQ"]:
                    kb.op("pool", lambda e, t_=t_: e.memset(t_[64:128, :, :], 0.0), w=[S_["K"]("zpad")])
            if nxt is not None:
                prefetch_x(nxt)
            s5g = s5_tile_gen()
            pc_tick = [0]; pc_done = [0]
            pc_quota = -(-len(pc_jobs) // max(1, NPRE - 1)) if NPRE > 1 else 0
            s5_alive = [True]

            def s5_step():
                if s5_alive[0]:
                    try:
                        next(s5g)
                    except StopIteration:
                        s5_alive[0] = False

            for hp0 in range(0, 8, 2):
                gens = [hp_gen(hp0, SCR[0]), hp_gen(hp0 + 1, SCR[1])]
                alive = [True, True]
                while any(alive):
                    for gi_ in range(2):
                        if alive[gi_]:
                            try:
                                next(gens[gi_])
                            except StopIteration:
                                alive[gi_] = False
                    s5_step()
                    pc_tick[0] += 1
                    if (not full) and (not last_pre) and pc_tick[0] % 4 == 0 and pc_done[0] < pc_quota:
                        pc_emit(1)
                        pc_done[0] += 1
            while s5_alive[0]:
                s5_step()

            if not sample and full:
                pass
            if STOP < 4:
                return
            K = lambda n: ("ARENA", n)
            if not full or STOP < 5:
                return
            Z = AR_[:, 0:1024].rearrange("p (a b) -> p a b", a=8); Z2 = AR_[:, 1024:2048].rearrange("p (a b) -> p a b", a=8)
            Z3 = AR_[:, 2048:3072].rearrange("p (a b) -> p a b", a=8)
            kb.fence({"ARENA", "ARENA2", "WS", "WSS"}, "dve", lambda e: e.memset(SMALL[:, 0:1], 0.0))
            sd_b = COLS[:, CO_SD:CO_SD + 8].unsqueeze(2).broadcast_to([128, 8, T])
            kb.tt("dve", Z[:], UF[:], sd_b, ALU.mult, r=[("UF", i) for i in range(8)] + ["COLS"], w=[K("z")])
            kb.tt("dve", Z[:], Z[:], YS, ALU.add, r=[K("z")] + XFk, w=[K("z")])
            kb.tt("pool", Z2[:], Z[:], Z[:], ALU.mult, r=[K("z")], w=[K("z2")])
            kb.ts("dve", Z2[:], Z2[:], 0.044715, ALU.mult, 1.0, ALU.add, r=[K("z2")], w=[K("z2")])
            kb.tt("dve", Z2[:], Z2[:], Z[:], ALU.mult, r=[K("z2"), K("z")], w=[K("z2")])
            kb.act(Z2[:], Z2[:], AF.Sigmoid, r=[K("z2")], w=[K("z2")], scale=2.0 * math.sqrt(2.0 / math.pi))
            Zb = AR_[:, 3072:3072 + 512].bitcast(BF16).rearrange("p (a b) -> p a b", a=8)
            kb.tt("dve", Zb, Z[:], Z2[:], ALU.mult, r=[K("z2"), K("z")], w=[K("zb")])
            for blk in range(4):
                wv, wk = colblock(glu_b_, 8, blk * 256, 256)
                for j in range(2):
                    oc = blk * 2 + j
                    b = nb()
                    for dk in range(8):
                        kb.mm(bank(b)[:, 0:T], wv[:, dk, j * 128:(j + 1) * 128], Zb[:, dk, :], start=(dk == 0), stop=(dk == 7), r=[wk, K("zb")], w=[("ps", b)])
                    kb.act(Z3[:, oc, :], bank(b)[:, 0:T], AF.Sigmoid, r=[("ps", b), "COLS"], w=[K("z3")], bias=col(CO_GLB, oc))
            kb.tt("dve", OF[:, 8:16, :], Zb, Z3[:], ALU.mult, r=[K("zb"), K("z3")], w=[("OF", 8)])
            if STOP < 6:
                return
            OFk = [("OF", i) for i in range(9)]
            for dk2 in range(8):
                wv, wk = rowblock(w_out_b, dk2 * 256, 2)
                for j in range(2):
                    dk = dk2 * 2 + j
                    for cb in range(4):
                        kb.mm(bank(cb), OF[:, dk, :], wv[:, j, cb * 512:(cb + 1) * 512], start=(dk == 0), stop=(dk == 15), r=[wk] + OFk, w=[("ps", cb)])
            for cb in range(4):
                kb.stt(XT[:, cb * 512:(cb + 1) * 512], XT[:, cb * 512:(cb + 1) * 512], ALPHA, bank(cb), ALU.mult, ALU.add, r=[kx, ("ps", cb)], w=[kx])
            layer_norm_tm(XT, 2, 3, kx)
            to_fm(XT, kx)
            if STOP < 7:
                return
            kb.fence({"PR", "ARENA", "ARENA2", "WS", "WSS"}, "dve", lambda e: e.memset(SMALL[:, 0:1], 0.0))
            GF = PR[:].rearrange("p a b c -> p (a b c)")[:, 0:11 * T // 2].bitcast(BF16).rearrange("p (a b) -> p a b", a=11)
            H1 = AR_[:, 0:T]
            for qt in range(4):
                for fl in range(11):
                    f = qt * 11 + fl
                    if fl % 2 == 0:
                        ncol = 256 if fl < 10 else 128
                        w1v, w1k = colblock(w1_b, 16, f * 128, ncol)
                        w3v, w3k = colblock(w3_b, 16, f * 128, ncol)
                    j = fl % 2
                    ba, bb_ = 4 + (fl % 2) * 2, 5 + (fl % 2) * 2
                    for dk in range(16):
                        kb.mm(bank(ba)[:, 0:T], w1v[:, dk, j * 128:(j + 1) * 128], XF[:, dk, :], start=(dk == 0), stop=(dk == 15), r=[w1k] + XFk, w=[("ps", ba)])
                    for dk in range(16):
                        kb.mm(bank(bb_)[:, 0:T], w3v[:, dk, j * 128:(j + 1) * 128], XF[:, dk, :], start=(dk == 0), stop=(dk == 15), r=[w3k] + XFk, w=[("ps", bb_)])
                    h1 = AR_[:, (fl % 2) * T:(fl % 2 + 1) * T]
                    kb.act(h1, bank(ba)[:, 0:T], AF.Silu, r=[("ps", ba)], w=[K(("h1", fl % 2))])
                    kb.tt("dve", GF[:, fl, :], h1, bank(bb_)[:, 0:T], ALU.mult, r=[K(("h1", fl % 2)), ("ps", bb_)], w=[("PR", ("gf", fl))])
                for fl2 in range(6):
                    nch = 2 if fl2 < 5 else 1
                    wv, wk = rowblock(w2_b, (qt * 11 + fl2 * 2) * 128, nch)
                    for j in range(nch):
                        fl = fl2 * 2 + j
                        f = qt * 11 + fl
                        for cb in range(4):
                            kb.mm(bank(cb), GF[:, fl, :], wv[:, j, cb * 512:(cb + 1) * 512], start=(f == 0), stop=(f == 43),
                                  r=[wk, ("PR", ("gf", fl))], w=[("ps", cb)])
            for cb in range(4):
                kb.stt(XT[:, cb * 512:(cb + 1) * 512], XT[:, cb * 512:(cb + 1) * 512], ALPHA, bank(cb), ALU.mult, ALU.add, r=[kx, ("ps", cb)], w=[kx])
            layer_norm_tm(XT, 4, 5, kx)
            kb.dma("sp", yout, XT[:], r=[kx], stream=("out", "y", tix % 2))

        descs = []
        for ti in range(NTP):
            full = ti >= NPRE
            descs.append((xp[ti * T:(ti + 1) * T, :], ti, full, None, yp[(ti - NPRE) * T:(ti - NPRE + 1) * T, :] if full else None, ti == NPRE, ti == NPRE - 1))
        for si in range(NSAMP_TILES):
            descs.append((xs[si * T:(si + 1) * T, :], NTP + si, True, [2 * si, 2 * si + 1], ys_o[si * T:(si + 1) * T, :], False, False))
        for ti in range(NTP):
            tile(*descs[ti], nxt=descs[ti + 1] if ti + 1 < len(descs) else None)
        for h_ in range(8):
            kb.dma("sp", wkv_o[NSQ - 1, :, h_ * 64:(h_ + 1) * 64], STT_[:, h_, :], r=[("ST", h_)], stream=("out", "wkv", h_))
        kb.dma("sp", s5_o[NSQ - 1].rearrange("a p b -> p a b"), HC[:], r=HCk, stream=("out", "s5"))
        for si in range(NSAMP_TILES):
            tile(*descs[NTP + si], nxt=descs[NTP + si + 1] if NTP + si + 1 < len(descs) else None)
        for sq in range(NSQ):
            b = nb()
            kb.tr(bank(b)[0:27, 0:128], SHO[:, :, sq], ident, r=["SHO", "CONST"], w=[("ps", b)])
            kb.cp("dve", ROWS[0:27, 0, sq * 128:(sq + 1) * 128], bank(b)[0:27, 0:128], r=[("ps", b)], w=[("ROWS", 0)])
        kb.dma("sp", shift_o.rearrange("s a b -> a s b"), ROWS[0:27, 0, 0:NSQ * 128].rearrange("p (s b) -> p s b", s=NSQ), r=[("ROWS", 0)], stream=("out", "sh"))
        kb.emit()
    return nc


def _fmcols(v, n):
    v = np.asarray(v, np.float32).reshape(-1)
    out = np.zeros(n * 128, np.float32)
    out[:v.size] = v
    return out.reshape(n, 128).T


def _consts():
    c = np.zeros((128, NCONST), np.float32)
    c[:, K_ID:K_ID + 128] = np.eye(128)
    bo = np.zeros((128, 128)); bo[:64, :64] = 1; bo[64:, 64:] = 1
    c[:, K_BO:K_BO + 128] = bo
    ma = np.triu(np.ones((64, 64)), 1); mr = np.triu(np.ones((64, 64)), 0)
    c[:, K_MASK:K_MASK + 128] = np.block([[ma, mr], [ma, mr]])
    c[0:64, K_LT:K_LT + 64] = np.tril(np.ones((64, 64)), -1)
    cm = np.ones(T); cm[::L] = 0
    c[:, K_CM:K_CM + T] = cm
    c[:, K_TAU:K_TAU + 64] = np.arange(1, 65)
    r0 = np.ones(64); r0[0] = 0
    c[:, K_R0:K_R0 + 64] = r0
    c[:, K_PI2] = math.pi / 2
    return c


def _pair_of(fc, q):
    return q * 8 + fc


def prep_shared(inp):
    f = lambda k: np.asarray(inp[k], np.float32)
    sh = {}
    sh["w_in"] = np.ascontiguousarray(f("w_in")[0]); sh["w_out"] = np.ascontiguousarray(f("w_out")[0])
    sh["w1"] = np.ascontiguousarray(f("ffn_w1")[0]); sh["w3"] = np.ascontiguousarray(f("ffn_w3")[0]); sh["w2"] = np.ascontiguousarray(f("ffn_w2")[0])
    sh["glu_w"] = np.ascontiguousarray(f("glu_w")[0])
    lora = np.zeros((3, 128, 1024), np.float32)
    lora[0, :64] = f("w_lora_up")[0]; lora[0, 64:] = f("a_lora_up")[0]
    lora[1] = f("g_lora_up")[0][:128]; lora[2, :32] = f("g_lora_up")[0][128:]
    sh["lora"] = lora
    cols = np.zeros((128, NCOL), np.float32)
    cols[:, CO_MU:CO_MU + 27] = _fmcols(f("mu_shift")[0], 27)
    for off, key in ((CO_W0, "w0"), (CO_A0, "a0"), (CO_KK, "k_k"), (CO_KA, "k_a"), (CO_RK, "r_k"), (CO_GG, "gn_g"), (CO_GB, "gn_b"), (CO_SD, "s5_d"), (CO_GLB, "glu_b")):
        cols[:, off:off + 8] = _fmcols(f(key)[0], 8)
    sh["cols"] = cols
    sh["consts"] = _consts()
    sh["rows"] = np.stack([f("ln_in_g"), f("ln_in_b"), f("ln1_g")[0], f("ln1_b")[0], f("ln2_g")[0], f("ln2_b")[0]]).astype(np.float32)
    a_re, a_im, ldt = f("s5_a_re")[0], f("s5_a_im")[0], f("s5_log_dt")[0]
    b_re, b_im, c_re, c_im = f("s5_b_re")[0], f("s5_b_im")[0], f("s5_c_re")[0], f("s5_c_im")[0]
    s5a = np.zeros((128, 3, 32), np.float32); s5q = np.zeros((128, 3, 8, 128), np.float32)
    bblk = np.zeros((128, 2, 8, 128), np.float32); cblk = np.zeros((128, 2, 32, 32), np.float32)
    for fc in range(8):
        for q in range(4):
            pi = _pair_of(fc, q)
            for g2 in range(2):
                g = 8 * fc + 2 * q + g2
                s5a[g2 * 64:(g2 + 1) * 64, 0, pi] = a_re[g]; s5a[g2 * 64:(g2 + 1) * 64, 1, pi] = a_im[g]; s5a[g2 * 64:(g2 + 1) * 64, 2, pi] = ldt[g]
                s5q[32 * q:32 * q + 32, 0, fc, g2 * 64:(g2 + 1) * 64] = a_re[g][None]
                s5q[32 * q:32 * q + 32, 1, fc, g2 * 64:(g2 + 1) * 64] = a_im[g][None]
                s5q[32 * q:32 * q + 32, 2, fc, g2 * 64:(g2 + 1) * 64] = ldt[g]
                bblk[32 * q + 16 * g2:32 * q + 16 * g2 + 16, 0, fc, g2 * 64:(g2 + 1) * 64] = b_re[g].T
                bblk[32 * q + 16 * g2:32 * q + 16 * g2 + 16, 1, fc, g2 * 64:(g2 + 1) * 64] = b_im[g].T
                cblk[g2 * 64:(g2 + 1) * 64, 0, pi, g2 * 16:(g2 + 1) * 16] = c_re[g].T
                cblk[g2 * 64:(g2 + 1) * 64, 1, pi, g2 * 16:(g2 + 1) * 16] = c_im[g].T
    sh["s5a"] = s5a.reshape(128, 96); sh["s5q"] = s5q.reshape(128, 3072)
    sh["bblk"] = bblk.reshape(128, 2048); sh["cblk"] = cblk.reshape(128, 2048)
    return sh


def s5_state_to_dev(re, im):
    out = np.zeros((2, 128, 32), np.float32)
    for fc in range(8):
        for q in range(4):
            for g2 in range(2):
                g = 8 * fc + 2 * q + g2
                out[0, g2 * 64:(g2 + 1) * 64, _pair_of(fc, q)] = re[g]
                out[1, g2 * 64:(g2 + 1) * 64, _pair_of(fc, q)] = im[g]
    return out


def s5_state_from_dev(d):
    re = np.zeros((64, 64), np.float32); im = np.zeros((64, 64), np.float32)
    for fc in range(8):
        for q in range(4):
            for g2 in range(2):
                g = 8 * fc + 2 * q + g2
                re[g] = d[0, g2 * 64:(g2 + 1) * 64, _pair_of(fc, q)]
                im[g] = d[1, g2 * 64:(g2 + 1) * 64, _pair_of(fc, q)]
    return re, im


def wkv_to_dev(s):
    return np.ascontiguousarray(s.reshape(8, 2, 64, 64).transpose(1, 3, 0, 2).reshape(128, 512))


def wkv_from_dev(d):
    return np.ascontiguousarray(d.reshape(2, 64, 8, 64).transpose(2, 0, 3, 1).reshape(16, 64, 64))


def run(inputs, n_cores, seq_len, n_samp_per_core):
    sh = prep_shared(inputs)
    half = seq_len // 2
    NH = half // T
    nst = n_samp_per_core // 2
    nc = build(NH, NH, nst)
    xp = np.asarray(inputs["x_prompt"], np.float32); xs = np.asarray(inputs["x_sample"], np.float32)
    cs = np.asarray(inputs["cache_shift"], np.float32)[0]; sw = np.asarray(inputs["state_wkv"], np.float32)[0]
    sre = np.asarray(inputs["state_s5_re"], np.float32)[0]; sim = np.asarray(inputs["state_s5_im"], np.float32)[0]
    in_maps = []
    for c in range(n_cores):
        s, h = c // 2, c % 2
        m = dict(sh)
        own = xp[s, h * half:(h + 1) * half]
        pre = xp[s, 0:half]
        m["xp"] = np.ascontiguousarray(np.concatenate([pre, own], 0))
        m["flag"] = np.full((128, 1), float(h), np.float32)
        sq = range(c * n_samp_per_core, (c + 1) * n_samp_per_core)
        m["xs"] = np.ascontiguousarray(xs[list(sq)].reshape(-1, D))
        csh = np.zeros((n_samp_per_core, 27 * 128), np.float32); csh[:, :NSH] = cs[list(sq), 0]
        m["cshift"] = csh
        m["swkv"] = np.stack([wkv_to_dev(sw[i]) for i in sq])
        m["s5st"] = np.stack([s5_state_to_dev(sre[i], sim[i]) for i in sq])
        in_maps.append(m)
    res = run_bass_kernel_spmd(nc, in_maps, core_ids=list(range(n_cores)))
    R = res.results
    B = n_cores // 2
    NS = n_cores * n_samp_per_core
    y_p = np.zeros((B, seq_len, D), np.float32); y_s = np.zeros((NS, 64, D), np.float32)
    sh_p = np.zeros((1, B, 1, NSH), np.float32); wkv_p = np.zeros((1, B, 16, 64, 64), np.float32)
    re_p = np.zeros((1, B, 64, 64), np.float32); im_p = np.zeros((1, B, 64, 64), np.float32)
    sh_s = np.zeros((1, NS, 1, NSH), np.float32); wkv_s = np.zeros((1, NS, 16, 64, 64), np.float32)
    re_s = np.zeros((1, NS, 64, 64), np.float32); im_s = np.zeros((1, NS, 64, 64), np.float32)
    for c in range(n_cores):
        s, h = c // 2, c % 2
        r = R[c]
        y_p[s, h * half:(h + 1) * half] = r["yp"]
        y_s[c * n_samp_per_core:(c + 1) * n_samp_per_core] = r["ys"].reshape(n_samp_per_core, 64, D)
        for i in range(n_samp_per_core):
            gi = c * n_samp_per_core + i
            sh_s[0, gi, 0] = r["shift_o"][i].reshape(-1)[:NSH]
            wkv_s[0, gi] = wkv_from_dev(r["wkv_o"][i])
            re_s[0, gi], im_s[0, gi] = s5_state_from_dev(r["s5_o"][i])
        if h == 1:
            i = n_samp_per_core
            sh_p[0, s, 0] = r["shift_o"][i].reshape(-1)[:NSH]
            wkv_p[0, s] = wkv_from_dev(r["wkv_o"][i])
            re_p[0, s], im_p[0, s] = s5_state_from_dev(r["s5_o"][i])
    return (y_p, y_s, sh_p, wkv_p, re_p, im_p, sh_s, wkv_s, re_s, im_s)


def kernel(**inputs):
    return run(inputs, 8, 4096, 4)
```
